# Optimizing a Trainium2 kernel written in Bass

```python
import math
import jax, jax.numpy as jnp
from jax import lax
import numpy as np

D_MODEL = 1024
BATCH = 16
SEQ = 2048
DEPTH = 4

GRID_W = 64
CTX_LEN = 256
MIXERS = ("s5", "conv")
N_MIX = len(MIXERS)
S5_GROUP = 16
S5_GROUPS = D_MODEL // S5_GROUP
S5_STATE = 64
N_DIR = 2
CONV_WIDTH = 31
CONV_HALF = CONV_WIDTH // 2
D_FF = 4 * D_MODEL
N_S5_LAYERS = sum(1 for _i in range(DEPTH) if MIXERS[_i % N_MIX] == "s5")
N_CONV_LAYERS = DEPTH - N_S5_LAYERS
DN_ALPHA = (2.0 * DEPTH) ** 0.25
DN_BETA = (8.0 * DEPTH) ** -0.25
LN_EPS = 1e-5
DT_MIN = 1e-3
DT_MAX = 1e-1
POS_TEMP = 10000.0
LAMBDA_RE_MAX = -1e-4

kernel_name = "hybrid_s5_conformer_dit_trunk"


def layer_norm(x, g, b):
    xf = x.astype(jnp.float32)
    mu = jnp.mean(xf, axis=-1, keepdims=True)
    var = jnp.mean(jnp.square(xf - mu), axis=-1, keepdims=True)
    y = (xf - mu) * lax.rsqrt(var + LN_EPS) * g.astype(jnp.float32) + b.astype(jnp.float32)
    return y.astype(x.dtype)


def modulate(x, shift, scale):
    return x * (1 + scale) + shift


def sincos_1d(pos, dim):
    quarter = dim // 2
    omega = POS_TEMP ** (-jnp.arange(quarter, dtype=jnp.float32) / quarter)
    ang = pos[:, None] * omega[None, :]
    return jnp.concatenate([jnp.sin(ang), jnp.cos(ang)], axis=-1)


def grid_pos_embed(rows, dim):
    row_idx = jnp.repeat(jnp.arange(rows), GRID_W).astype(jnp.float32)
    col_idx = jnp.tile(jnp.arange(GRID_W), rows).astype(jnp.float32)
    return jnp.concatenate([sincos_1d(row_idx, dim // 2), sincos_1d(col_idx, dim // 2)], axis=-1)


def s5_discretise(lam_re, lam_im, log_dt, b_re, b_im):
    lr = jnp.minimum(lam_re.astype(jnp.float32), LAMBDA_RE_MAX)
    li = lam_im.astype(jnp.float32)
    dt = jnp.exp(log_dt.astype(jnp.float32))[:, None]
    mag = jnp.exp(lr * dt)
    ab_re = mag * jnp.cos(li * dt)
    ab_im = mag * jnp.sin(li * dt)
    den = lr * lr + li * li
    nr = ab_re - 1.0
    ni = ab_im
    coef_re = (nr * lr + ni * li) / den
    coef_im = (ni * lr - nr * li) / den
    br = b_re.astype(jnp.float32)
    bi = b_im.astype(jnp.float32)
    bb_re = coef_re[..., None] * br - coef_im[..., None] * bi
    bb_im = coef_re[..., None] * bi + coef_im[..., None] * br
    return ab_re, ab_im, bb_re, bb_im


def _scan_op(e1, e2):
    ar1, ai1, br1, bi1 = e1
    ar2, ai2, br2, bi2 = e2
    return (ar2 * ar1 - ai2 * ai1,
            ar2 * ai1 + ai2 * ar1,
            ar2 * br1 - ai2 * bi1 + br2,
            ar2 * bi1 + ai2 * br1 + bi2)


def s5_scan(u, lam_re, lam_im, log_dt, b_re, b_im, h0s):
    length = u.shape[1]
    states = []
    for d in range(N_DIR):
        reverse = d == 1
        ab_re, ab_im, bb_re, bb_im = s5_discretise(lam_re[d], lam_im[d], log_dt[d], b_re[d], b_im[d])
        bu_re = jnp.einsum("blgc,gpc->blgp", u, bb_re)
        bu_im = jnp.einsum("blgc,gpc->blgp", u, bb_im)
        if h0s is not None:
            h0_re, h0_im = h0s[d]
            edge = -1 if reverse else 0
            bu_re = bu_re.at[:, edge].add(ab_re * h0_re - ab_im * h0_im)
            bu_im = bu_im.at[:, edge].add(ab_re * h0_im + ab_im * h0_re)
        a_re = jnp.broadcast_to(ab_re[None, None], (1, length) + ab_re.shape)
        a_im = jnp.broadcast_to(ab_im[None, None], (1, length) + ab_im.shape)
        _, _, h_re, h_im = lax.associative_scan(_scan_op, (a_re, a_im, bu_re, bu_im), reverse=reverse, axis=1)
        states.append((h_re, h_im))
    return states


def s5_readout(u, states, c_re, c_im, d_skip, w_glu, b_glu, out_dtype):
    y = d_skip.astype(jnp.float32).reshape(S5_GROUPS, S5_GROUP) * u
    for d, (h_re, h_im) in enumerate(states):
        y = y + jnp.einsum("blgp,gcp->blgc", h_re, c_re[d].astype(jnp.float32)) \
              - jnp.einsum("blgp,gcp->blgc", h_im, c_im[d].astype(jnp.float32))
    bsz, length = u.shape[0], u.shape[1]
    z = jax.nn.gelu(y.reshape(bsz, length, D_MODEL), approximate=False).astype(out_dtype)
    zz = z @ w_glu + b_glu
    return zz[..., :D_MODEL] * jax.nn.sigmoid(zz[..., D_MODEL:])


def to_groups(h):
    return h.astype(jnp.float32).reshape(h.shape[0], h.shape[1], S5_GROUPS, S5_GROUP)


def conv_module(h, w_pw1, b_pw1, w_dw, b_dw, ln_g, ln_b, w_pw2, b_pw2):
    a = h @ w_pw1 + b_pw1
    a = a[..., :D_MODEL] * jax.nn.sigmoid(a[..., D_MODEL:])
    a = lax.conv_general_dilated(a, w_dw[:, None, :].astype(a.dtype), window_strides=(1,),
                                 padding=[(CONV_HALF, CONV_HALF)],
                                 dimension_numbers=("NWC", "WIO", "NWC"),
                                 feature_group_count=D_MODEL) + b_dw
    a = jax.nn.silu(layer_norm(a, ln_g, ln_b))
    return a @ w_pw2 + b_pw2


def sq_relu_mlp(h, w1, w2):
    return jnp.square(jax.nn.relu(h @ w1)) @ w2


def setup_inputs(seed: int = 0) -> dict:
    key = jax.random.key(seed)
    ks = jax.random.split(key, 32)
    f32 = jnp.float32
    D, G, P, CH = D_MODEL, S5_GROUPS, S5_STATE, S5_GROUP
    nrm = lambda k, shape, s: jax.random.normal(k, shape, f32) * s
    x = nrm(ks[0], (BATCH, SEQ, D), 1.0)
    c = nrm(ks[1], (BATCH, D), 1.0)
    ctx = nrm(ks[2], (BATCH, CTX_LEN, D), 1.0)
    c_ctx = nrm(ks[3], (D,), 1.0)
    w_ada = nrm(ks[4], (DEPTH, D, 6 * D), D ** -0.5)
    b_ada = nrm(ks[5], (DEPTH, 6 * D), 0.02)
    ln_gain = 1.0 + nrm(ks[6], (DEPTH, 2, D), 0.02)
    ln_bias = nrm(ks[7], (DEPTH, 2, D), 0.02)
    n_idx = jnp.arange(P, dtype=f32)
    s5_lam_re = -0.5 + nrm(ks[8], (N_S5_LAYERS, N_DIR, G, P), 0.01)
    s5_lam_im = math.pi * n_idx + nrm(ks[9], (N_S5_LAYERS, N_DIR, G, P), 0.01)
    s5_log_dt = jax.random.uniform(ks[10], (N_S5_LAYERS, N_DIR, G), f32, math.log(DT_MIN), math.log(DT_MAX))
    s5_b_re = nrm(ks[11], (N_S5_LAYERS, N_DIR, G, P, CH), (2.0 * CH) ** -0.5)
    s5_b_im = nrm(ks[12], (N_S5_LAYERS, N_DIR, G, P, CH), (2.0 * CH) ** -0.5)
    s5_c_re = nrm(ks[13], (N_S5_LAYERS, N_DIR, G, CH, P), P ** -0.5)
    s5_c_im = nrm(ks[14], (N_S5_LAYERS, N_DIR, G, CH, P), P ** -0.5)
    s5_d = 1.0 + nrm(ks[15], (N_S5_LAYERS, D), 0.1)
    glu_out = nrm(ks[16], (N_S5_LAYERS, D, D), DN_BETA * D ** -0.5)
    glu_gate = nrm(ks[17], (N_S5_LAYERS, D, D), D ** -0.5)
    s5_w_glu = jnp.concatenate([glu_out, glu_gate], axis=-1)
    s5_b_glu = nrm(ks[18], (N_S5_LAYERS, 2 * D), 0.02)
    cv_w_pw1 = nrm(ks[19], (N_CONV_LAYERS, D, 2 * D), D ** -0.5)
    cv_b_pw1 = nrm(ks[20], (N_CONV_LAYERS, 2 * D), 0.02)
    cv_w_dw = nrm(ks[21], (N_CONV_LAYERS, CONV_WIDTH, D), CONV_WIDTH ** -0.5)
    cv_b_dw = nrm(ks[22], (N_CONV_LAYERS, D), 0.02)
    cv_ln_g = 1.0 + nrm(ks[23], (N_CONV_LAYERS, D), 0.02)
    cv_ln_b = nrm(ks[24], (N_CONV_LAYERS, D), 0.02)
    cv_w_pw2 = nrm(ks[25], (N_CONV_LAYERS, D, D), DN_BETA * D ** -0.5)
    cv_b_pw2 = nrm(ks[26], (N_CONV_LAYERS, D), 0.02)
    mlp_w1 = nrm(ks[27], (DEPTH, D, D_FF), D ** -0.5)
    mlp_w2 = nrm(ks[28], (DEPTH, D_FF, D), DN_BETA * D_FF ** -0.5)
    return {"x": x, "c": c, "ctx": ctx, "c_ctx": c_ctx,
            "w_ada": w_ada, "b_ada": b_ada, "ln_gain": ln_gain, "ln_bias": ln_bias,
            "s5_lam_re": s5_lam_re, "s5_lam_im": s5_lam_im, "s5_log_dt": s5_log_dt,
            "s5_b_re": s5_b_re, "s5_b_im": s5_b_im, "s5_c_re": s5_c_re, "s5_c_im": s5_c_im,
            "s5_d": s5_d, "s5_w_glu": s5_w_glu, "s5_b_glu": s5_b_glu,
            "cv_w_pw1": cv_w_pw1, "cv_b_pw1": cv_b_pw1, "cv_w_dw": cv_w_dw, "cv_b_dw": cv_b_dw,
            "cv_ln_g": cv_ln_g, "cv_ln_b": cv_ln_b, "cv_w_pw2": cv_w_pw2, "cv_b_pw2": cv_b_pw2,
            "mlp_w1": mlp_w1, "mlp_w2": mlp_w2}


def reference(x, c, ctx, c_ctx, w_ada, b_ada, ln_gain, ln_bias,
              s5_lam_re, s5_lam_im, s5_log_dt, s5_b_re, s5_b_im, s5_c_re, s5_c_im,
              s5_d, s5_w_glu, s5_b_glu,
              cv_w_pw1, cv_b_pw1, cv_w_dw, cv_b_dw, cv_ln_g, cv_ln_b, cv_w_pw2, cv_b_pw2,
              mlp_w1, mlp_w2):
    rows = x.shape[1] // GRID_W
    x = x + grid_pos_embed(rows, D_MODEL).astype(x.dtype)[None]
    cond = jax.nn.silu(c)
    cond_ctx = jax.nn.silu(c_ctx)
    kinds = [MIXERS[i % N_MIX] for i in range(DEPTH)]
    s5_j = 0
    cv_j = 0
    for i in range(DEPTH):
        kind = kinds[i]
        ctx_read_here = kind == "s5"
        ctx_needed_later = any(k == "s5" for k in kinds[i + 1:])
        use_ctx = ctx_read_here or ctx_needed_later
        mod = (cond @ w_ada[i] + b_ada[i])[:, None, :]
        sh1, sc1, g1, sh2, sc2, g2 = jnp.split(mod, 6, axis=-1)
        if use_ctx:
            mod_c = (cond_ctx @ w_ada[i] + b_ada[i])[None, None, :]
            csh1, csc1, cg1, csh2, csc2, cg2 = jnp.split(mod_c, 6, axis=-1)
            hc = modulate(ctx, csh1, csc1)
        h = modulate(x, sh1, sc1)
        if kind == "s5":
            j = s5_j
            s5_j += 1
            uc = to_groups(hc)
            states_c = s5_scan(uc, s5_lam_re[j], s5_lam_im[j], s5_log_dt[j], s5_b_re[j], s5_b_im[j], None)
            h0s = [(states_c[0][0][:, -1], states_c[0][1][:, -1]),
                   (states_c[1][0][:, 0], states_c[1][1][:, 0])]
            u = to_groups(h)
            states = s5_scan(u, s5_lam_re[j], s5_lam_im[j], s5_log_dt[j], s5_b_re[j], s5_b_im[j], h0s)
            mix = s5_readout(u, states, s5_c_re[j], s5_c_im[j], s5_d[j], s5_w_glu[j], s5_b_glu[j], x.dtype)
            if ctx_needed_later:
                mix_c = s5_readout(uc, states_c, s5_c_re[j], s5_c_im[j], s5_d[j], s5_w_glu[j], s5_b_glu[j], ctx.dtype)
        else:
            j = cv_j
            cv_j += 1
            cv_args = (cv_w_pw1[j], cv_b_pw1[j], cv_w_dw[j], cv_b_dw[j], cv_ln_g[j], cv_ln_b[j], cv_w_pw2[j], cv_b_pw2[j])
            mix = conv_module(h, *cv_args)
            if ctx_needed_later:
                mix_c = conv_module(hc, *cv_args)
        x = layer_norm(DN_ALPHA * x + g1 * mix, ln_gain[i, 0], ln_bias[i, 0])
        h = modulate(x, sh2, sc2)
        x = layer_norm(DN_ALPHA * x + g2 * sq_relu_mlp(h, mlp_w1[i], mlp_w2[i]), ln_gain[i, 1], ln_bias[i, 1])
        if ctx_needed_later:
            ctx = layer_norm(DN_ALPHA * ctx + cg1 * mix_c, ln_gain[i, 0], ln_bias[i, 0])
            hc2 = modulate(ctx, csh2, csc2)
            ctx = layer_norm(DN_ALPHA * ctx + cg2 * sq_relu_mlp(hc2, mlp_w1[i], mlp_w2[i]), ln_gain[i, 1], ln_bias[i, 1])
    return x
```

```python
import math
from contextlib import ExitStack
import numpy as np
import concourse.bass as bass
import concourse.mybir as mybir
from concourse.bass_utils import run_bass_kernel_spmd

F32 = mybir.dt.float32
BF16 = mybir.dt.bfloat16
I32 = mybir.dt.int32
AF = mybir.ActivationFunctionType
ALU = mybir.AluOpType

D = 1024
SEQ = 2048
CTX = 256
NTOK = SEQ + CTX
DEPTH = 4
NFC = 8
DFF = 4096
TWO_PI = 2.0 * math.pi
DN_ALPHA = (2.0 * DEPTH) ** 0.25
LN_EPS = 1e-5


import os
NOWAR = os.environ.get('NOWAR', '0') == '1'


class Prog:
    EP = 30000
    KD = 8

    def __init__(self, nc, es):
        self.nc, self.es = nc, es
        self.engs = ["pe", "act", "dve", "pool", "sp"]
        self.q = {e: [] for e in self.engs}
        self.cnt = {e: 0 for e in self.engs}
        self.dma_n = {e: 0 for e in self.engs}
        self.lastw = {}
        self.readers = {}
        self.known = {e: {} for e in self.engs}
        self.knownd = {e: {} for e in self.engs}

    def op(self, eng, fn, reads=(), writes=(), dma=False):
        psr = [k for k in reads if isinstance(k, tuple) and k[0] == "ps"]
        if psr:
            writes = list(writes) + psr
        raw, war = set(), set()
        for k in reads:
            t = self.lastw.get(k)
            if t is not None:
                raw.add(t)
        for k in writes:
            t = self.lastw.get(k)
            if t is not None:
                raw.add(t)
            for t in self.readers.get(k, ()):
                war.add(t)
        if dma:
            n = self.dma_n[eng]
            self.dma_n[eng] += 1
            tok = ("d", eng, n)
            if n >= self.KD:
                raw.add(("d", eng, n - self.KD))
        else:
            tok = ("c", eng, self.cnt[eng])
            self.cnt[eng] += 1
        waits = []
        for kind, deps in (("raw", raw), ("war", war)):
            for t in sorted(deps):
                if t[0] == "c":
                    _, e2, i2 = t
                    if e2 == eng and not dma:
                        if eng == "pe" or (kind == "war" and NOWAR):
                            continue
                    if self.known[eng].get(e2, -1) >= i2:
                        continue
                    self.known[eng][e2] = i2
                    waits.append(t)
                else:
                    _, q2, n2 = t
                    slot = (q2, n2 % self.KD)
                    if self.knownd[eng].get(slot, -1) >= n2:
                        continue
                    self.knownd[eng][slot] = n2
                    waits.append(t)
        for k in reads:
            self.readers.setdefault(k, []).append(tok)
        for k in writes:
            self.lastw[k] = tok
            self.readers[k] = []
        self.q[eng].append((fn, waits, tok))
        return tok

    def emit(self):
        nc, es = self.nc, self.es
        mil = {e: set() for e in self.engs}
        for e in self.engs:
            for fn, waits, tok in self.q[e]:
                for t in waits:
                    if t[0] == "c":
                        mil[t[1]].add(t[2])
        rank = {}
        nm = {}
        for e in self.engs:
            srt = sorted(mil[e])
            nm[e] = len(srt)
            for r, idx in enumerate(srt):
                rank[(e, idx)] = r
        psem = {e: [es.enter_context(nc.semaphore(f"p_{e}_{i}")) for i in range(nm[e] // self.EP + 1)]
                for e in self.engs}
        dsem = {e: [es.enter_context(nc.semaphore(f"d_{e}_{i}")) for i in range(self.KD)]
                for e in self.engs if self.dma_n[e] > 0}
        block = es.enter_context(nc.Block())
        self.stats = {e: (len(self.q[e]), nm[e], sum(len(w) for _, w, _ in self.q[e])) for e in self.engs}

        def resolve(t):
            if t[0] == "c":
                r = rank[(t[1], t[2])]
                return psem[t[1]][r // self.EP], (r % self.EP) + 1
            _, q2, n2 = t
            return dsem[q2][n2 % self.KD], 16 * (n2 // self.KD + 1)

        def run(e, eng):
            for fn, waits, tok in self.q[e]:
                for t in waits:
                    s, v = resolve(t)
                    eng.wait_ge(s, v)
                inst = fn(eng)
                if tok[0] == "c":
                    r = rank.get((e, tok[2]))
                    if r is not None:
                        inst.then_inc(psem[e][r // self.EP], 1)
                else:
                    inst.then_inc(dsem[e][tok[2] % self.KD], 16)
            n = self.dma_n[e]
            for j in range(max(0, n - self.KD), n):
                s, v = resolve(("d", e, j))
                eng.wait_ge(s, v)

        @block.tensor
        def _(eng):
            run("pe", eng)

        @block.scalar
        def _(eng):
            run("act", eng)

        @block.vector
        def _(eng):
            run("dve", eng)

        @block.gpsimd
        def _(eng):
            run("pool", eng)

        @block.sync
        def _(eng):
            run("sp", eng)


BLK = 512
AX = mybir.AxisListType

def c_bada(i): return i * 48
def c_lng(i, k): return 192 + (i * 2 + k) * 8
def c_lnb(i, k): return 256 + (i * 2 + k) * 8
def c_s5d(j): return 320 + j * 8
def c_bglu(j): return 336 + j * 16
def c_bpw1(j): return 368 + j * 16
def c_wdw(j, tap): return 400 + (j * 31 + tap) * 8
def c_bdw(j): return 896 + j * 8
def c_cvg(j): return 912 + j * 8
def c_cvb(j): return 928 + j * 8
def c_bpw2(j): return 944 + j * 8
NVEC = 1024


class V:
    __slots__ = ("ap", "keys")

    def __init__(self, ap, keys):
        self.ap, self.keys = ap, keys


class Buf:
    def __init__(self, AR, off, shape, dt):
        self.off, self.shape, self.dt = off, list(shape), dt
        self.esz = 2 if dt == BF16 else 4
        n = 1
        for d in shape:
            n *= d
        self.nbytes = n * self.esz
        assert off % 4 == 0 and self.nbytes % 4 == 0
        ap = AR[:, off // 4:(off + self.nbytes) // 4]
        if dt != F32:
            ap = ap.bitcast(dt)
        if len(shape) == 2:
            ap = ap.rearrange("p (a b) -> p a b", a=shape[0])
        elif len(shape) == 3:
            ap = ap.rearrange("p (a b c) -> p a b c", a=shape[0], b=shape[1])
        elif len(shape) == 4:
            ap = ap.rearrange("p (a b c d) -> p a b c d", a=shape[0], b=shape[1], c=shape[2])
        self.full = ap
        self.strides = []
        st = 1
        for d in reversed(shape):
            self.strides.insert(0, st)
            st *= d

    def _rng(self, idx, dims):
        lo = hi = 0
        for k, d in zip(idx, dims):
            st = self.strides[d]
            if isinstance(k, int):
                lo += k * st
                hi += k * st
            else:
                a, b, c = k.indices(self.shape[d])
                last = a + ((b - a - 1) // c) * c
                lo += a * st
                hi += last * st
        return lo, hi

    def __getitem__(self, idx):
        if not isinstance(idx, tuple):
            idx = (idx,)
        idx = tuple(idx) + (slice(None),) * (len(self.shape) - len(idx))
        return self.view(idx, 128)

    def view(self, idx, nparts=128, p0=0):
        ap = self.full[(slice(p0, p0 + nparts),) + tuple(idx)]
        keys = set()
        first = idx[0]
        if isinstance(first, int):
            firsts = [first]
        else:
            a, b, c = first.indices(self.shape[0])
            firsts = list(range(a, b, c))
        for f in firsts:
            lo, hi = self._rng((f,) + tuple(idx[1:]), range(len(self.shape)))
            b0 = (self.off + lo * self.esz) // BLK
            b1 = (self.off + (hi + 1) * self.esz - 1) // BLK
            for bb in range(b0, b1 + 1):
                keys.add(bb)
        return V(ap, list(keys))


def build_nc(plan=None, nseq=2, pos=True, dbg=False):
    if plan is None:
        plan = [(i, True, True) for i in range(DEPTH)]
    nc = bass.Bass("TRN2", target_bir_lowering=False)
    es = ExitStack()
    P = Prog(nc, es)

    def din(name, shape):
        return nc.dram_tensor(name, list(shape), F32, kind="ExternalInput").ap()

    xT_d = din("xT", [2, 128, NFC, SEQ])
    cT_d = din("cT", [2, 128, NFC, CTX])
    cond_d = din("condT", [128, NFC, 3])
    vec_d = din("vecT", [128, NVEC])
    wada_d = din("w_ada", [DEPTH, D, 6 * D])
    w1_d = din("mlp_w1", [DEPTH, D, DFF])
    w2_d = din("mlp_w2", [DEPTH, DFF, D])
    wglu_d = din("s5_w_glu", [2, D, 2 * D])
    wpw1_d = din("cv_w_pw1", [2, D, 2 * D])
    wpw2_d = din("cv_w_pw2", [2, D, D])
    lamre_d = din("lam_reT", [2, 128, 64])
    lamim_d = din("lam_imT", [2, 128, 64])
    logdt_d = din("log_dtT", [2, 128, 64])
    bre_d = din("b_reT", [2, 128, 64, 16])
    bim_d = din("b_imT", [2, 128, 64, 16])
    cre_d = din("c_reT", [2, 128, 64, 16])
    cim_d = din("c_imT", [2, 128, 64, 16])
    s5d_d = din("s5_dT", [2, 128, 64])
    y_d = nc.dram_tensor("yT", [2, 128, NFC, NTOK], F32, kind="ExternalOutput").ap()
    dbg_d = nc.dram_tensor("dbg", [128, 16384], F32, kind="ExternalOutput").ap() if dbg else None

    ARBYTES = 212480
    AR = es.enter_context(nc.sbuf_tensor("AR", [128, ARBYTES // 4], F32))
    PS = [es.enter_context(nc.psum_tensor(f"ps{i}", [128, 512], F32)) for i in range(8)]
    PSB = [p[:, :].bitcast(BF16) for p in PS]
    psrr = {}

    def ps_next(lo, hi):
        i = psrr.get((lo, hi), 0)
        psrr[(lo, hi)] = i + 1
        return lo + i % (hi - lo)

    def psv(i, a=0, b=512):
        return V(PS[i][:, a:b], [("ps", i)])

    def psvb(i, a=0, b=1024):
        return V(PSB[i][:, a:b], [("ps", i)])

    top = {"o": 0}

    def alloc(shape, dt=F32, at=None):
        off = top["o"] if at is None else at
        bf = Buf(AR, off, shape, dt)
        end = off + ((bf.nbytes + BLK - 1) // BLK) * BLK
        if at is None:
            top["o"] = end
        assert end <= ARBYTES, (end, ARBYTES)
        return bf

    def _k(*vs):
        out = []
        for v in vs:
            if isinstance(v, V):
                out += v.keys
        return out

    def _a(v):
        return v.ap if isinstance(v, V) else v

    def MM(out, lhsT, rhs, start, stop):
        P.op("pe", lambda e: e.matmul(out.ap, lhsT.ap, rhs.ap, start=start, stop=stop),
             reads=_k(lhsT, rhs), writes=_k(out))

    def TR(out, in_, idn):
        P.op("pe", lambda e: e.transpose(out=out.ap, in_=in_.ap, identity=idn.ap), reads=_k(in_, idn), writes=_k(out))

    def ACT(out, in_, func, scale=None, bias=None):
        kw = {}
        if scale is not None:
            kw["scale"] = _a(scale)
        if bias is not None:
            kw["bias"] = _a(bias)
        P.op("act", lambda e: e.activation(out=out.ap, in_=in_.ap, func=func, **kw),
             reads=_k(in_, scale, bias), writes=_k(out))

    def TT(eng, out, in0, in1, op):
        P.op(eng, lambda e: e.tensor_tensor(out=out.ap, in0=in0.ap, in1=in1.ap, op=op), reads=_k(in0, in1), writes=_k(out))

    def TS(eng, out, in0, s1, s2, op0, op1=None):
        if op1 is None:
            P.op(eng, lambda e: e.tensor_scalar(out=out.ap, in0=in0.ap, scalar1=_a(s1), scalar2=None, op0=op0),
                 reads=_k(in0, s1), writes=_k(out))
        else:
            P.op(eng, lambda e: e.tensor_scalar(out=out.ap, in0=in0.ap, scalar1=_a(s1), scalar2=_a(s2), op0=op0, op1=op1),
                 reads=_k(in0, s1, s2), writes=_k(out))

    def STT(eng, out, in0, scalar, in1, op0, op1):
        P.op(eng, lambda e: e.scalar_tensor_tensor(out=out.ap, in0=in0.ap, scalar=_a(scalar), in1=in1.ap, op0=op0, op1=op1),
             reads=_k(in0, scalar, in1), writes=_k(out))

    def CP(eng, out, in_):
        if eng == "act":
            P.op(eng, lambda e: e.activation(out=out.ap, in_=in_.ap, func=AF.Copy), reads=_k(in_), writes=_k(out))
        else:
            P.op(eng, lambda e: e.tensor_copy(out=out.ap, in_=in_.ap), reads=_k(in_), writes=_k(out))

    def RED(eng, out, in_, op=ALU.add):
        P.op(eng, lambda e: e.tensor_reduce(out=out.ap, in_=in_.ap, axis=AX.X, op=op), reads=_k(in_), writes=_k(out))

    def RECIP(out, in_):
        P.op("dve", lambda e: e.reciprocal(out=out.ap, in_=in_.ap), reads=_k(in_), writes=_k(out))

    def MEMSET(eng, out, val):
        P.op(eng, lambda e: e.memset(out.ap, val), writes=_k(out))

    def IOTA(out, pattern, base, cm):
        P.op("pool", lambda e: e.iota(out.ap, pattern, base=base, channel_multiplier=cm), writes=_k(out))

    def DMA(queue, out, in_):
        P.op(queue, lambda e: e.dma_start(out=_a(out), in_=_a(in_)), reads=_k(in_), writes=_k(out), dma=True)

    def DUMP(v, c0, n):
        if dbg:
            DMA("sp", dbg_d[:, c0:c0 + n], v)

    def bc(v, shape):
        return V(v.ap.to_broadcast(list(shape)), v.keys)

    def re(v, pattern, **kw):
        return V(v.ap.rearrange(pattern, **kw), v.keys)

    XT = alloc([NFC, NTOK])
    VEC = alloc([NVEC])
    MOD = alloc([DEPTH, 6, NFC, 3])
    IDN = alloc([128])
    IDNB = alloc([128], BF16)
    ONES = alloc([128])
    CST = alloc([8])
    OM = alloc([2])
    EXPT = alloc([3, 5, 8])
    MSKF = alloc([128])
    AR2P = alloc([2, 64, 2])
    AI2P = alloc([2, 64, 2])
    MSKB = alloc([128])
    ARENA0 = top["o"]

    TILES = [(0, 256)] + [(256 + 512 * i, 512) for i in range(4)]

    def vcol(c):
        return VEC[c:c + 1]

    tI = alloc([128], I32, at=ARENA0)
    IOTA(tI[:], [[1, 128]], 0, -1)
    CP("dve", IDN[:], tI[:])
    P.op("dve", lambda e: e.tensor_single_scalar(out=IDN[:].ap, in_=IDN[:].ap, scalar=0.0, op=ALU.is_equal),
         reads=IDN[:].keys, writes=IDN[:].keys)
    CP("dve", IDNB[:], IDN[:])
    MEMSET("dve", ONES[:], 1.0)
    MEMSET("dve", CST[0:1], LN_EPS / (DN_ALPHA * DN_ALPHA))
    MEMSET("dve", CST[1:2], LN_EPS)
    MEMSET("dve", CST[2:3], 1.0)
    tI2 = alloc([2], I32, at=ARENA0 + 1024)
    IOTA(tI2[:], [[128, 2]], 0, 1)
    CP("dve", OM[:], tI2[:])
    ACT(OM[:], OM[:], AF.Exp, scale=-math.log(10000.0) / 256.0)
    DMA("sp", VEC[:], vec_d[:, :])

    def sin_reduce(a, qf, qi, eng="dve"):
        TS(eng, qf, a, 1.0 / TWO_PI, None, ALU.mult)
        CP(eng, qi, qf)
        CP(eng, qf, qi)
        STT(eng, a, qf, -TWO_PI, a, ALU.mult, ALU.add)
        TS(eng, qf, a, 0.0, None, ALU.is_lt)
        STT(eng, a, qf, TWO_PI, a, ALU.mult, ALU.add)
        TS(eng, a, a, -math.pi, math.pi, ALU.add, ALU.min)

    MODS_BASE = ARENA0 + 112640
    mods_steps = []

    def compute_mods():
        o = MODS_BASE
        CF = alloc([NFC, 3], F32, at=o)
        CB = alloc([NFC, 3], BF16, at=o + 512)
        WAD = [alloc([NFC, 768], BF16, at=o + 1024)]
        DMA("sp", CF[:], cond_d[:, :, :])
        ACT(CB[:], CF[:], AF.Silu)
        state = {"n": 0, "pb": None}

        def piece_step(i, piece):
            def run():
                if piece == 0:
                    state["pb"] = ps_next(6, 8)
                pb = state["pb"]
                w = WAD[0]
                state["n"] += 1
                DMA("pool", w[:], wada_d[i].rearrange("(kc p) n -> p kc n", p=128)[:, :, piece * 768:(piece + 1) * 768])
                for o6 in range(6):
                    oc = piece * 6 + o6
                    for kc in range(NFC):
                        MM(psv(pb, oc * 3, oc * 3 + 3), w[kc, o6 * 128:(o6 + 1) * 128], CB[kc, :], kc == 0, kc == NFC - 1)
                if piece == 7:
                    mo = re(MOD[i], "p m f w -> p (m f) w")
                    TT("dve", mo, re(psv(pb, 0, 144), "p (o w) -> p o w", w=3),
                       bc(re(VEC[c_bada(i):c_bada(i) + 48], "p (o w) -> p o w", w=1), [128, 48, 3]), ALU.add)
                    for m in (1, 4):
                        TS("dve", MOD[i, m], MOD[i, m], 1.0, None, ALU.add)
                    for m in (2, 5):
                        TS("dve", MOD[i, m], MOD[i, m], 1.0 / DN_ALPHA, None, ALU.mult)
                    if i == DEPTH - 1:
                        DUMP(re(MOD[:], "p a b c d -> p (a b c d)"), 0, 576)
            return run
        for i in range(DEPTH):
            for piece in range(8):
                mods_steps.append(piece_step(i, piece))

    def mods_tick(n):
        for _ in range(n):
            if mods_steps:
                mods_steps.pop(0)()

    def mod(i, m, fc, who):
        return MOD[i, m, fc, who:who + 1]

    def load_seq(b):
        for fc in range(NFC):
            DMA("sp", XT[fc, 0:CTX], cT_d[b, :, fc, :])
            DMA("sp", XT[fc, CTX:NTOK], xT_d[b, :, fc, :])
        if not pos:
            return
        o = ARENA0
        PEt = alloc([512], F32, at=o)
        posr = alloc([512], F32, at=o + 2048)
        posc = alloc([512], F32, at=o + 4096)
        posi = alloc([512], I32, at=o + 6144)
        ang = alloc([512], F32, at=o + 8192)
        angq = alloc([512], F32, at=o + 10240)
        angi = alloc([512], I32, at=o + 12288)
        IOTA(posi[:], [[0, 8], [1, 64]], 0, 0)
        CP("dve", posc[:], posi[:])
        for t in range(4):
            IOTA(posi[:], [[1, 8], [0, 64]], t * 8, 0)
            CP("dve", posr[:], posi[:])
            for fc in range(NFC):
                src = posr if fc < 4 else posc
                shift = math.pi + (math.pi / 2 if (fc // 2) % 2 == 1 else 0.0)
                j = fc % 2
                TS("dve", ang[:], src[:], OM[j:j + 1], shift, ALU.mult, ALU.add)
                sin_reduce(ang[:], angq[:], angi[:])
                ACT(PEt[:], ang[:], AF.Sin)
                xs = XT[fc, CTX + t * 512:CTX + (t + 1) * 512]
                TT("dve", xs, xs, PEt[:], ALU.add)

    def store_seq(b):
        for fc in range(NFC):
            DMA("sp", y_d[b, :, fc, :], XT[fc, :])

    def layer_norm(tiles, gcol, bcol, eps_idx, base, src=None, dst=None, func=AF.Identity):
        src = src or (lambda fc, t0, n: XT[fc, t0:t0 + n])
        srcall = None
        SQ = alloc([NFC, 512], F32, at=base)
        S1 = alloc([512], F32, at=base + 16384)
        S2 = alloc([512], F32, at=base + 18432)
        MEAN = alloc([512], F32, at=base + 20480)
        RSTD = alloc([512], F32, at=base + 22528)
        for (t0, n) in tiles:
            for fc in range(NFC):
                ACT(SQ[fc, 0:n], src(fc, t0, n), AF.Square)
            xs = [src(fc, t0, n) for fc in range(NFC)]
            TT("pool", S1[0:n], xs[0], xs[1], ALU.add)
            for fc in range(2, NFC):
                TT("pool", S1[0:n], S1[0:n], xs[fc], ALU.add)
            RED("dve", S2[0:n], re(SQ[:, 0:n], "p f n -> p n f"))
            p1 = ps_next(6, 8)
            p2 = ps_next(6, 8)
            MM(psv(p1, 0, n), ONES[:], S1[0:n], True, True)
            MM(psv(p2, 0, n), ONES[:], S2[0:n], True, True)
            TS("dve", MEAN[0:n], psv(p1, 0, n), 1.0 / D, None, ALU.mult)
            TT("dve", S1[0:n], MEAN[0:n], MEAN[0:n], ALU.mult)
            STT("dve", S2[0:n], psv(p2, 0, n), 1.0 / D, S1[0:n], ALU.mult, ALU.subtract)
            ACT(S2[0:n], S2[0:n], AF.Sqrt, bias=CST[eps_idx:eps_idx + 1])
            RECIP(RSTD[0:n], S2[0:n])
            for fc in range(NFC):
                xv = src(fc, t0, n)
                ne = "dve"
                TT(ne, SQ[fc, 0:n], xv, MEAN[0:n], ALU.subtract)
                TT(ne, SQ[fc, 0:n], SQ[fc, 0:n], RSTD[0:n], ALU.mult)
                ov = dst(fc, t0, n) if dst else xv
                ACT(ov, SQ[fc, 0:n], func, scale=vcol(gcol + fc), bias=vcol(bcol + fc))

    def mlp_block(i, b, tiles, fuse_ln1=False):
        o = ARENA0
        HM = alloc([NFC, NTOK], BF16, at=o); o += 36864
        W1S = [alloc([NFC, 512], BF16, at=o + s * 8192) for s in range(2)]; o += 16384
        W2S = [alloc([4, D], BF16, at=o + s * 8192) for s in range(2)]; o += 16384
        HID = [alloc([4, 512], BF16, at=o + s * 4096) for s in range(2)]; o += 8192
        RL = [alloc([512], F32, at=o + s * 2048) for s in range(2)]; o += 4096
        EV = [alloc([512], F32, at=o + s * 2048) for s in range(2)]; o += 4096
        lnbase = o
        assert lnbase + 24576 <= ARBYTES

        def load_w(q):
            s = q % 2
            DMA("pool", W1S[s][:], w1_d[i].rearrange("(kc p) n -> p kc n", p=128)[:, :, q * 512:(q + 1) * 512])
            DMA("pool", W2S[s][:], w2_d[i, q * 512:(q + 1) * 512, :].rearrange("(jj p) n -> p jj n", p=128))

        def prep(t0, n):
            who = 2 if t0 < CTX else b
            if fuse_ln1:
                layer_norm([(t0, n)], c_lng(i, 0), c_lnb(i, 0), 0, lnbase)
            for fc in range(NFC):
                ACT(HM[fc, t0:t0 + n], XT[fc, t0:t0 + n], AF.Identity, scale=mod(i, 4, fc, who), bias=mod(i, 3, fc, who))

        load_w(0)
        work = [(q, t0, n) for q in range(8) for (t0, n) in tiles]
        nt = len(tiles)
        rl_cnt = [0]

        def Hstage(k):
            q, t0, n = work[k]
            s = q % 2
            hs = k % 2
            for jj in range(4):
                pb = ps_next(0, 3)
                for kc in range(NFC):
                    MM(psv(pb, 0, n), W1S[s][kc, jj * 128:(jj + 1) * 128], HM[kc, t0:t0 + n], kc == 0, kc == NFC - 1)
                r = RL[rl_cnt[0] % 2]
                rl_cnt[0] += 1
                ACT(r[0:n], psv(pb, 0, n), AF.Relu)
                TT("pool" if jj % 2 else "dve", HID[hs][jj, 0:n], r[0:n], r[0:n], ALU.mult)

        def Ostage(k):
            q, t0, n = work[k]
            s = q % 2
            who = 2 if t0 < CTX else b
            hs = k % 2
            for f in range(NFC):
                pb = ps_next(3, 6)
                for jj in range(4):
                    MM(psv(pb, 0, n), W2S[s][jj, f * 128:(f + 1) * 128], HID[hs][jj, 0:n], jj == 0, jj == 3)
                xv = XT[f, t0:t0 + n]
                if f % 2 == 0:
                    STT("dve", xv, psv(pb, 0, n), mod(i, 5, f, who), xv, ALU.mult, ALU.add)
                else:
                    ev = EV[(f // 2) % 2]
                    ACT(ev[0:n], psv(pb, 0, n), AF.Identity, scale=mod(i, 5, f, who))
                    TT("pool", xv, xv, ev[0:n], ALU.add)

        prep(*tiles[0])
        if nt > 1:
            prep(*tiles[1])
        for k in range(len(work) + 1):
            if k < len(work):
                q, t0, n = work[k]
                ti = k % nt
                if q == 0 and ti + 2 < nt:
                    prep(*tiles[ti + 2])
                Hstage(k)
            if k >= 1:
                Ostage(k - 1)
                q1, t01, n1 = work[k - 1]
                if (k - 1) % nt == 0 and q1 + 1 < 8:
                    load_w(q1 + 1)
                if q1 == 7:
                    layer_norm([(t01, n1)], c_lng(i, 1), c_lnb(i, 1), 0, lnbase)

    def acol(tok):
        return tok + 15 if tok < CTX else tok + 45

    def conv_block(i, b, tiles, do_ln1=True):
        j = i // 2
        o = ARENA0
        NA = 2364
        AB = alloc([NFC, NA], BF16, at=o)
        oA = o + 38912
        WP1 = alloc([NFC, 2 * D], BF16, at=oA)
        HMt = [alloc([NFC, 512], BF16, at=oA + 32768 + s * 8192) for s in range(2)]
        SIG = [alloc([512], F32, at=oA + 49152 + s * 2048) for s in range(2)]
        WP2 = alloc([NFC, D], BF16, at=oA + 69632)
        DG = [alloc([31, 128], BF16, at=oA + s * 8192) for s in range(2)]
        CO = alloc([NFC, 512], F32, at=oA + 16384)
        NB = alloc([NFC, 512], BF16, at=oA + 32768)
        TMP = [alloc([512], F32, at=oA + 40960 + s * 2048) for s in range(2)]
        lnbase = oA + 45056
        DMA("pool", WP1[:], wpw1_d[j].rearrange("(kc p) n -> p kc n", p=128))
        DMA("pool", WP2[:], wpw2_d[j].rearrange("(kc p) n -> p kc n", p=128))
        MEMSET("dve", AB[:, 0:15], 0.0)
        MEMSET("dve", AB[:, 271:301], 0.0)
        MEMSET("dve", AB[:, 2349:2364], 0.0)
        n_ = 0
        for (t0, n) in tiles:
            who = 2 if t0 < CTX else b
            hm = HMt[n_ % 2]
            for fc in range(NFC):
                ACT(hm[fc, 0:n], XT[fc, t0:t0 + n], AF.Identity, scale=mod(i, 1, fc, who), bias=mod(i, 0, fc, who))
            for oc in range(NFC):
                pa = ps_next(0, 3)
                pg = ps_next(3, 6)
                for kc in range(NFC):
                    MM(psv(pa, 0, n), WP1[kc, oc * 128:(oc + 1) * 128], hm[kc, 0:n], kc == 0, kc == NFC - 1)
                for kc in range(NFC):
                    MM(psv(pg, 0, n), WP1[kc, D + oc * 128:D + (oc + 1) * 128], hm[kc, 0:n], kc == 0, kc == NFC - 1)
                sg = SIG[n_ % 2]
                n_ += 1
                ACT(sg[0:n], psv(pg, 0, n), AF.Sigmoid, bias=vcol(c_bpw1(j) + 8 + oc))
                c0 = acol(t0)
                STT("dve", AB[oc, c0:c0 + n], psv(pa, 0, n), vcol(c_bpw1(j) + oc), sg[0:n], ALU.add, ALU.mult)
        m_ = 0
        for (t0, n) in tiles:
            who = 2 if t0 < CTX else b
            c0 = acol(t0)
            for fc in range(NFC):
                dg = DG[m_ % 2]
                m_ += 1
                wv = V(VEC.full[:, c_wdw(j, 0) + fc:c_wdw(j, 31) + fc:8], VEC[c_wdw(j, 0):c_wdw(j, 31)].keys)
                TT("dve", dg[:], bc(re(IDNB[:], "p (o x) -> p o x", o=1), [128, 31, 128]),
                   bc(re(wv, "p (t o) -> p t o", o=1), [128, 31, 128]), ALU.mult)
                pb = ps_next(0, 3)
                for tap in range(31):
                    MM(psv(pb, 0, n), dg[tap, :], AB[fc, c0 + tap - 15:c0 + tap - 15 + n], tap == 0, tap == 30)
                ACT(CO[fc, 0:n], psv(pb, 0, n), AF.Identity, bias=vcol(c_bdw(j) + fc))
            layer_norm([(0, n)], c_cvg(j), c_cvb(j), 1, lnbase,
                       src=lambda fc, t0_, n_: CO[fc, 0:n_], dst=lambda fc, t0_, n_: NB[fc, 0:n_], func=AF.Silu)
            for f in range(NFC):
                pb = ps_next(3, 6)
                for kc in range(NFC):
                    MM(psv(pb, 0, n), WP2[kc, f * 128:(f + 1) * 128], NB[kc, 0:n], kc == 0, kc == NFC - 1)
                tm = TMP[f % 2]
                ACT(tm[0:n], psv(pb, 0, n), AF.Identity, bias=vcol(c_bpw2(j) + f))
                xv = XT[f, t0:t0 + n]
                STT("dve", xv, tm[0:n], mod(i, 2, f, who), xv, ALU.mult, ALU.add)
        if do_ln1:
            layer_norm(tiles, c_lng(i, 0), c_lnb(i, 0), 0, lnbase)

    NH = 4
    KCH = NTOK // 32
    NKAP = NTOK // 8
    KOFF = 23
    KBS = [(0, 128), (128, 128), (256, 32)]

    def s5_consts():
        ti = alloc([3, 5, 8], I32, at=ARENA0 + 2048)
        MEMSET("dve", ti[:], 0)
        P.op("pool", lambda e: e.iota(ti.view((0, slice(None), slice(None)), 64, 0).ap, [[-8, 5], [-1, 8]], base=32, channel_multiplier=0),
             writes=ti[:].keys)
        P.op("pool", lambda e: e.iota(ti.view((0, slice(None), slice(None)), 64, 64).ap, [[8, 5], [1, 8]], base=0, channel_multiplier=0),
             writes=ti[:].keys)
        P.op("pool", lambda e: e.iota(ti.view((1, slice(None), slice(None)), 64, 0).ap, [[8, 5], [1, 8]], base=0, channel_multiplier=0),
             writes=ti[:].keys)
        P.op("pool", lambda e: e.iota(ti.view((1, slice(None), slice(None)), 64, 64).ap, [[-8, 5], [-1, 8]], base=32, channel_multiplier=0),
             writes=ti[:].keys)
        P.op("pool", lambda e: e.iota(ti.view((2, 0, slice(None)), 64, 0).ap, [[-1, 8]], base=0, channel_multiplier=0),
             writes=ti[:].keys)
        P.op("pool", lambda e: e.iota(ti.view((2, 0, slice(None)), 64, 64).ap, [[1, 8]], base=0, channel_multiplier=0),
             writes=ti[:].keys)
        CP("dve", EXPT[:], ti[:])
        mi = alloc([128], I32, at=ARENA0 + 4096)
        IOTA(re(mi[:], "p (t c) -> p t c", c=16), [[16, 8], [0, 16]], 0, -1)
        CP("dve", MSKF[:], mi[:])
        CP("dve", MSKB[:], mi[:])
        TS("dve", MSKF[:], MSKF[:], -15.0, None, ALU.is_ge)
        TS("dve", MSKB[:], MSKB[:], 0.0, None, ALU.is_le)

    def pw_tables(dst_r, dst_i, lrdt_v, lidt_v, ex_v, shp, tb):
        G, M = shp
        ANG, A2, QF, QI = tb
        lr_b = bc(re(lrdt_v, "p (g o) -> p g o", o=1), [128, G, M])
        li_b = bc(re(lidt_v, "p (g o) -> p g o", o=1), [128, G, M])
        ex_b = bc(re(ex_v, "p (o m) -> p o m", o=1), [128, G, M])
        TT("dve", ANG[:], li_b, ex_b, ALU.mult)
        TT("dve", A2[:], lr_b, ex_b, ALU.mult)
        ACT(dst_i[:], A2[:], AF.Exp)
        TS("dve", A2[:], ANG[:], math.pi + TWO_PI * KOFF + math.pi / 2, None, ALU.add)
        sin_reduce(A2[:], QF[:], QI[:])
        ACT(A2[:], A2[:], AF.Sin)
        TT("dve", dst_r[:], dst_i[:], A2[:], ALU.mult)
        TS("dve", A2[:], ANG[:], math.pi + TWO_PI * KOFF, None, ALU.add)
        sin_reduce(A2[:], QF[:], QI[:])
        ACT(A2[:], A2[:], AF.Sin)
        TT("dve", dst_i[:], dst_i[:], A2[:], ALU.mult)

    def cmul_bc(dr, di, ar, ai, br, bi, tmp, neg_im=False):
        TT("dve", dr, ar, br, ALU.mult)
        TT("dve", tmp, ai, bi, ALU.mult)
        TT("dve", dr, dr, tmp, ALU.subtract)
        TT("dve", di, ar, bi, ALU.mult)
        TT("dve", tmp, ai, br, ALU.mult)
        TT("dve", di, di, tmp, ALU.add)

    WIN_D = [nc.dram_tensor(f"win_s{j}", [128, 64, 2, 4, 128], BF16).ap() for j in range(2)]
    WOUT_D = [nc.dram_tensor(f"wout_s{j}", [128, 64, 2, 4, 128], BF16).ap() for j in range(2)]
    IM_D = [nc.dram_tensor(f"im_s{j}", [128, 64, 7, 128], BF16).ap() for j in range(2)]

    def bcg(v, n_in):
        return bc(re(v, "p (g o) -> p g o", o=1), [128, v.ap.shape[1], n_in])

    def cmul_e(eng, dr, di, ar, ai, br, bi, tmp):
        TT(eng, dr, ar, br, ALU.mult)
        TT(eng, tmp, ai, bi, ALU.mult)
        TT(eng, dr, dr, tmp, ALU.subtract)
        TT(eng, di, ar, bi, ALU.mult)
        TT(eng, tmp, ai, br, ALU.mult)
        TT(eng, di, di, tmp, ALU.add)

    def s5_tables(j):
        o = ARENA0
        G8 = 8
        M = 40
        TAB = alloc([9, 64], F32, at=o); o += 2560
        LRDT, LIDT, COR, COI, T1, T2, T3, T4, DST = range(9)
        PQ = alloc([64, 2], F32, at=o); o += 512
        PQ2 = alloc([64, 2], F32, at=o); o += 512
        EX1 = alloc([2], F32, at=o); o += 512
        MA = 88
        SL = [o + k * 3072 for k in range(6)]
        o += 6 * 3072
        PWt = [alloc([G8, MA], F32, at=SL[k]) for k in range(3)]
        PWi = alloc([G8, MA], I32, at=SL[3])
        Pr = alloc([G8, MA], F32, at=SL[4])
        Pi = alloc([G8, MA], F32, at=SL[5])
        Er = alloc([G8, MA], F32, at=SL[0])
        Ei = alloc([G8, MA], F32, at=SL[1])
        ETMP = alloc([G8, MA], F32, at=SL[2])
        Pin = alloc([G8, M], F32, at=o); o += 1536
        BR = alloc([G8, 16], F32, at=o)
        BI = alloc([G8, 16], F32, at=o + 512)
        CR = alloc([G8, 16], F32, at=o + 1024)
        CI = alloc([G8, 16], F32, at=o + 1536)
        o += 2048
        D1 = alloc([128], F32, at=o)
        D2 = alloc([128], F32, at=o + 512)
        o += 1024
        G4 = 4
        LRE = alloc([G4, 5, 128], F32, at=o); o += 10240
        LIM = alloc([G4, 5, 128], F32, at=o); o += 10240
        LT = alloc([G4, 5, 128], F32, at=o); o += 10240
        LT2 = alloc([G4, 5, 128], F32, at=o); o += 10240
        L0R = alloc([G8, 128], F32, at=o); o += 4096
        L0I = alloc([G8, 128], F32, at=o); o += 4096
        L0T = alloc([G8, 128], F32, at=o); o += 4096
        WIN = alloc([G8, 2, 4, 128], BF16, at=o); o += 16384
        IM = alloc([G8, 7, 128], BF16, at=o); o += 14336
        assert o <= MODS_BASE, (o, MODS_BASE)

        DMA("sp", TAB[T1], lamre_d[j, :, :])
        DMA("sp", TAB[T2], lamim_d[j, :, :])
        DMA("sp", TAB[T3], logdt_d[j, :, :])
        DMA("sp", TAB[DST], s5d_d[j, :, :])
        TS("dve", TAB[T1], TAB[T1], -1e-4, None, ALU.min)
        ACT(TAB[T3], TAB[T3], AF.Exp)
        TT("dve", TAB[LRDT], TAB[T1], TAB[T3], ALU.mult)
        TT("dve", TAB[LIDT], TAB[T2], TAB[T3], ALU.mult)
        MEMSET("dve", EX1[0:1], 1.0)
        MEMSET("dve", EX1[1:2], 32.0)
        tb = [alloc([64, 2], F32, at=LRE.off + k * 512) for k in range(3)] + [alloc([64, 2], I32, at=LRE.off + 1536)]
        pw_tables(PQ, PQ2, TAB[LRDT], TAB[LIDT], EX1[:], (64, 2), tb)
        a1r = re(PQ[:, 0:1], "p g o -> p (g o)")
        a1i = re(PQ2[:, 0:1], "p g o -> p (g o)")
        TS("dve", TAB[T4], a1r, -1.0, None, ALU.add)
        TT("dve", TAB[COR], TAB[T4], TAB[T1], ALU.mult)
        TT("dve", TAB[COI], a1i, TAB[T2], ALU.mult)
        TT("dve", TAB[COR], TAB[COR], TAB[COI], ALU.add)
        TT("dve", TAB[COI], a1i, TAB[T1], ALU.mult)
        TT("dve", TAB[T4], TAB[T4], TAB[T2], ALU.mult)
        TT("dve", TAB[COI], TAB[COI], TAB[T4], ALU.subtract)
        TT("dve", TAB[T1], TAB[T1], TAB[T1], ALU.mult)
        TT("dve", TAB[T2], TAB[T2], TAB[T2], ALU.mult)
        TT("dve", TAB[T1], TAB[T1], TAB[T2], ALU.add)
        RECIP(TAB[T1], TAB[T1])
        TT("dve", TAB[COR], TAB[COR], TAB[T1], ALU.mult)
        TT("dve", TAB[COI], TAB[COI], TAB[T1], ALU.mult)
        aTr = PQ[:, 1:2]
        aTi = PQ2[:, 1:2]
        CP("dve", AR2P[j, :, 0:1], aTr)
        CP("dve", AR2P[j, :, 1:2], aTr)
        CP("dve", AI2P[j, :, 1:2], aTi)
        TS("dve", AI2P[j, :, 0:1], aTi, -1.0, None, ALU.mult)

        def e4(t, n):
            return bc(re(t, "p g (m o) -> p g m o", o=1), [128, t.ap.shape[1], n, 16])

        def b4(t, n):
            return bc(re(t, "p g (o c) -> p g o c", o=1), [128, t.ap.shape[1], n, 16])

        def f4(t):
            return re(t, "p g d (s c) -> p g (d s) c", c=16)

        for fc in range(NFC):
            f0 = fc * 8
            DMA("sp", BR[:], bre_d[j, :, f0:f0 + 8, :])
            DMA("sp", BI[:], bim_d[j, :, f0:f0 + 8, :])
            DMA("sp", CR[:], cre_d[j, :, f0:f0 + 8, :])
            DMA("sp", CI[:], cim_d[j, :, f0:f0 + 8, :])
            pw_tables(Pr, Pi, TAB[LRDT, f0:f0 + 8], TAB[LIDT, f0:f0 + 8], re(EXPT[:], "p a d s -> p (a d s)")[:, 0:MA] if False else V(EXPT.full.rearrange("p a d s -> p (a d s)")[:, 0:MA], EXPT[:].keys),
                      (G8, MA), PWt[0:3] + [PWi])
            TS("dve", Pin[:], Pi[:, 40:80], -1.0, None, ALU.mult)
            cmul_bc(Er[:], Ei[:], Pr[:], Pi[:], bcg(TAB[COR, f0:f0 + 8], MA), bcg(TAB[COI, f0:f0 + 8], MA), ETMP[:])
            for h0 in (0, 4):
                hs_ = slice(h0, h0 + 4)
                TT("dve", f4(LRE[:]), e4(Er[hs_, 0:40], M), b4(BR[hs_, :], M), ALU.mult)
                TT("dve", f4(LT[:]), e4(Ei[hs_, 0:40], M), b4(BI[hs_, :], M), ALU.mult)
                TT("dve", f4(LRE[:]), f4(LRE[:]), f4(LT[:]), ALU.subtract)
                TT("pool", f4(LT2[:]), e4(Ei[hs_, 0:40], M), b4(BR[hs_, :], M), ALU.mult)
                TT("dve", f4(LIM[:]), e4(Er[hs_, 0:40], M), b4(BI[hs_, :], M), ALU.mult)
                TT("dve", f4(LIM[:]), f4(LIM[:]), f4(LT2[:]), ALU.add)
                for g4 in range(4):
                    for ri, LL in enumerate((LRE, LIM)):
                        pb = ps_next(3, 6)
                        for sg in range(4):
                            TR(psv(pb, sg * 128, (sg + 1) * 128), LL[g4, sg, :], IDN[:])
                        CP("act", WIN[h0 + g4, ri], re(psv(pb), "p (s n) -> p s n", s=4))
                mods_tick(1)
            DMA("sp", V(WIN_D[j][:, f0:f0 + 8], [("dr", "win", j, fc)]), WIN[:])
            cmul_e("dve", re(L0R[:], "p g (s c) -> p g s c", c=16), re(L0I[:], "p g (s c) -> p g s c", c=16),
                   e4(Er[:, 80:88], 8), e4(Ei[:, 80:88], 8), b4(BR[:], 8), b4(BI[:], 8), re(L0T[:], "p g (s c) -> p g s c", c=16))
            for h0 in (0, 4):
                hs_ = slice(h0, h0 + 4)
                TT("dve", f4(LRE[:]), b4(CR[hs_, :], M), e4(Pr[hs_, 40:80], M), ALU.mult)
                TT("dve", f4(LT[:]), b4(CI[hs_, :], M), e4(Pi[hs_, 40:80], M), ALU.mult)
                TT("dve", f4(LRE[:]), f4(LRE[:]), f4(LT[:]), ALU.subtract)
                TT("pool", f4(LT2[:]), b4(CI[hs_, :], M), e4(Pr[hs_, 40:80], M), ALU.mult)
                TT("dve", f4(LIM[:]), b4(CR[hs_, :], M), e4(Pin[hs_, :], M), ALU.mult)
                TT("dve", f4(LIM[:]), f4(LIM[:]), f4(LT2[:]), ALU.subtract)
                CP("act", WIN[hs_, 0], LRE[:, 0:4, :])
                CP("act", WIN[hs_, 1], LIM[:, 0:4, :])
                for g4 in range(4):
                    gl = h0 + g4
                    g = f0 + gl
                    pF = ps_next(3, 6)
                    pB = ps_next(3, 6)
                    for dl in range(4):
                        for (pbk, p0, slot) in ((pF, 0, dl), (pB, 64, 4 - dl)):
                            outv = psv(pbk, dl * 128, (dl + 1) * 128)
                            MM(outv, V(L0R.full[p0:p0 + 64, gl, :], L0R[gl].keys), V(LRE.full[p0:p0 + 64, g4, slot, :], LRE[g4].keys), True, False)
                            MM(outv, V(L0I.full[p0:p0 + 64, gl, :], L0I[gl].keys), V(LIM.full[p0:p0 + 64, g4, slot, :], LIM[g4].keys), False, True)
                    TT("dve", D1[:], psv(pF, 0, 128), MSKF[:], ALU.mult)
                    TT("dve", D2[:], psv(pB, 0, 128), MSKB[:], ALU.mult)
                    TT("dve", D1[:], D1[:], D2[:], ALU.add)
                    STT("dve", IM[gl, 0, :], IDN[:], TAB[DST, g:g + 1], D1[:], ALU.mult, ALU.add)
                    CP("act", IM[gl, 1:4, :], re(psv(pF, 128, 512), "p (d x) -> p d x", d=3))
                    CP("act", IM[gl, 4:7, :], re(psv(pB, 128, 512), "p (d x) -> p d x", d=3))
                mods_tick(1)
            DMA("sp", V(WOUT_D[j][:, f0:f0 + 8], [("dr", "wout", j, fc)]), WIN[:])
            DMA("sp", V(IM_D[j][:, f0:f0 + 8], [("dr", "im", j, fc)]), IM[:])

    def s5_block(i, b, tiles, do_ln1=True):
        j = i // 2
        o = ARENA0
        ZT = alloc([NFC, NTOK], BF16, at=o); o += 36864
        HS = alloc([64, KCH + 1, 2], F32, at=o); o += 37376
        PQ = alloc([64, 2], F32, at=o); o += 512
        PQ2 = alloc([64, 2], F32, at=o); o += 512
        PQb = alloc([64, 2], F32, at=o); o += 512
        PQ2b = alloc([64, 2], F32, at=o); o += 512
        HMf = alloc([NTOK], BF16, at=o); o += 4608
        Z0 = alloc([3, 8, 128], BF16, at=o); o += 6144
        U = alloc([8, NKAP], BF16, at=o); o += 4608
        WINS = [alloc([2, 4, 128], BF16, at=o + k * 2048) for k in range(4)]; o += 8192
        IMS = [alloc([7, 128], BF16, at=o + k * 2048) for k in range(2)]; o += 4096
        HB = [alloc([2, KCH], BF16, at=o + k * 512) for k in range(2)]; o += 1024
        YG = alloc([8, NKAP], BF16, at=o); o += 4608
        Z1 = alloc([3, 8, 128], BF16, at=o); o += 6144
        assert o <= ARBYTES, o
        wg0 = ARENA0 + 36864
        WG = alloc([NFC, 2 * D], BF16, at=wg0)
        SIG = [alloc([512], F32, at=wg0 + 32768 + s * 2048) for s in range(2)]
        MIX = [alloc([512], F32, at=wg0 + 36864 + s * 2048) for s in range(2)]
        lnbase = wg0 + 40960
        assert lnbase + 24576 <= ARBYTES
        AR2 = AR2P
        AI2 = AI2P

        v0 = HS.view((slice(None), 0, slice(None)), 64, 0)
        MEMSET("dve", V(v0.ap, v0.keys + ["hsf"]), 0.0)
        v0 = HS.view((slice(None), KCH, slice(None)), 64, 64)
        MEMSET("dve", V(v0.ap, v0.keys + ["hsb"]), 0.0)

        def make_U(fc):
            for (t0, n, who) in ((0, CTX, 2), (CTX, SEQ, b)):
                ACT(HMf[t0:t0 + n], XT[fc, t0:t0 + n], AF.Identity, scale=mod(i, 1, fc, who), bias=mod(i, 0, fc, who))
            for kb, (k0, nk) in enumerate(KBS):
                pb = ps_next(0, 3)
                for tl in range(8):
                    src = HMf[k0 * 8 + tl:k0 * 8 + 8 * nk:8]
                    TR(V(PSB[pb][0:nk, tl * 128:(tl + 1) * 128], [("ps", pb)]), src, IDNB[:])
                z0v = Z0.view((kb, slice(None), slice(None)), nk, 0)
                CP("act", V(z0v.ap.rearrange("p g (t c) -> p g t c", c=16), z0v.keys),
                   V(PSB[pb][0:nk, :].rearrange("p (t g c) -> p g t c", t=8, g=8), [("ps", pb)]))
            for g3 in range(0, 8, 3):
                gn = min(3, 8 - g3)
                pb = ps_next(0, 3)
                for gl in range(g3, g3 + gn):
                    for kb, (k0, nk) in enumerate(KBS):
                        src = Z0.view((kb, gl, slice(None)), nk, 0)
                        c0 = (gl - g3) * NKAP + k0
                        TR(V(PSB[pb][:, c0:c0 + nk], [("ps", pb)]), src, V(IDNB.full[0:nk, 0:nk], IDNB[:].keys))
                CP("dve", U[g3:g3 + gn, :], V(PSB[pb][:, 0:gn * NKAP].rearrange("p (g k) -> p g k", g=gn), [("ps", pb)]))

        nw = 0
        for fc in range(NFC):
            make_U(fc)
            f0 = fc * 8
            for gl in range(8):
                g = f0 + gl
                W = WINS[nw % 4]
                nw += 1
                DMA("sp", W[:], V(WIN_D[j][:, g], [("dr", "win", j, fc)]))
                pb = ps_next(3, 6)
                for ri in range(2):
                    c0 = ri * KCH
                    for sh in range(4):
                        MM(psv(pb, c0, c0 + KCH), W[ri, sh, :], U[gl, sh:NKAP:4], sh == 0, sh == 3)
                src4 = PS[pb][:, 0:2 * KCH].rearrange("p (g r k) -> p g k r", g=1, r=2)
                P.op("act", lambda e, src4=src4, g=g: e.activation(out=HS.full[0:64, g:g + 1, 1:KCH + 1, :], in_=src4[0:64], func=AF.Copy),
                     reads=[("ps", pb)], writes=HS[g:g + 1].keys + ["hsf"])
                P.op("dve", lambda e, src4=src4, g=g: e.tensor_copy(out=HS.full[64:128, g:g + 1, 0:64, :], in_=src4[64:128, :, 8:72, :]),
                     reads=[("ps", pb)], writes=HS[g:g + 1].keys + ["hsb"])
                P.op("dve", lambda e, src4=src4, g=g: e.tensor_copy(out=HS.full[64:128, g:g + 1, 64:72, :], in_=src4[64:128, :, 0:8, :]),
                     reads=[("ps", pb)], writes=HS[g:g + 1].keys + ["hsb"])

        for step in range(KCH):
            for (eng, p0, prevc, curc, pq, pq2) in (("dve", 0, step, step + 1, PQ, PQ2), ("pool", 64, KCH - step, KCH - 1 - step, PQb, PQ2b)):
                prev = HS.full[p0:p0 + 64, :, prevc, :]
                cur = HS.full[p0:p0 + 64, :, curc, :]
                ar2 = AR2.full[p0:p0 + 64, j]
                ai2 = AI2.full[p0:p0 + 64, j]
                a = pq.full[p0:p0 + 64]
                q = pq2.full[p0:p0 + 64]
                hk = ["hsf" if p0 == 0 else "hsb"]
                P.op(eng, lambda e, a=a, ar2=ar2, prev=prev: e.tensor_tensor(out=a, in0=ar2, in1=prev, op=ALU.mult),
                     reads=hk + AR2[j].keys, writes=pq[:].keys)
                P.op(eng, lambda e, q=q, ai2=ai2, prev=prev: e.tensor_tensor(out=q[:, :, 0:1], in0=ai2[:, :, 0:1], in1=prev[:, :, 1:2], op=ALU.mult),
                     reads=hk + AI2[j].keys, writes=pq2[:].keys)
                P.op(eng, lambda e, q=q, ai2=ai2, prev=prev: e.tensor_tensor(out=q[:, :, 1:2], in0=ai2[:, :, 1:2], in1=prev[:, :, 0:1], op=ALU.mult),
                     reads=hk + AI2[j].keys, writes=pq2[:].keys)
                P.op(eng, lambda e, a=a, q=q: e.tensor_tensor(out=a, in0=a, in1=q, op=ALU.add),
                     reads=pq[:].keys + pq2[:].keys, writes=pq[:].keys)
                P.op(eng, lambda e, a=a, cur=cur: e.tensor_tensor(out=cur, in0=cur, in1=a, op=ALU.add),
                     reads=pq[:].keys + hk, writes=hk)

        ng = 0
        for fc in range(NFC):
            make_U(fc)
            f0 = fc * 8
            for gl in range(8):
                g = f0 + gl
                W = WINS[nw % 4]
                nw += 1
                IMg = IMS[ng % 2]
                hb = HB[ng % 2]
                ng += 1
                DMA("sp", W[:], V(WOUT_D[j][:, g], [("dr", "wout", j, fc)]))
                DMA("sp", IMg[:], V(IM_D[j][:, g], [("dr", "im", j, fc)]))
                hsk = HS[g:g + 1].keys + ["hsf", "hsb"]
                hbk = hb[:].keys
                P.op("act", lambda e, g=g, hb=hb: e.activation(out=hb.full[0:64], in_=HS.full[0:64, g, 0:KCH, :].rearrange("p k r -> p r k"), func=AF.Copy),
                     reads=hsk, writes=hbk)
                P.op("dve", lambda e, g=g, hb=hb: e.tensor_copy(out=hb.full[64:128, :, 8:72], in_=HS.full[64:128, g, 1:65, :].rearrange("p k r -> p r k")),
                     reads=hsk, writes=hbk)
                P.op("dve", lambda e, g=g, hb=hb: e.tensor_copy(out=hb.full[64:128, :, 0:8], in_=HS.full[64:128, g, 65:73, :].rearrange("p k r -> p r k")),
                     reads=hsk, writes=hbk)
                py = ps_next(3, 6)
                for th in range(4):
                    outv = V(PS[py][:, th:NKAP:4], [("ps", py)])
                    for sh in range(4):
                        dl = th - sh
                        blk = 0 if dl == 0 else (dl if dl > 0 else 3 - dl)
                        MM(outv, IMg[blk, :], U[gl, sh:NKAP:4], sh == 0, False)
                    MM(outv, W[0, th, :], hb[0, :], False, False)
                    MM(outv, W[1, th, :], hb[1, :], False, True)
                ACT(YG[gl, :], psv(py, 0, NKAP), AF.Gelu)
            for kb, (k0, nk) in enumerate(KBS):
                pb = ps_next(0, 3)
                for gl in range(8):
                    TR(V(PSB[pb][0:nk, gl * 128:(gl + 1) * 128], [("ps", pb)]), YG[gl, k0:k0 + nk], IDNB[:])
                src = PSB[pb][0:nk, :].rearrange("p (g t c) -> p t g c", g=8, t=8)
                dstv = Z1.view((kb, slice(None), slice(None)), nk, 0)
                P.op("act", lambda e, src=src, dstv=dstv: e.activation(out=dstv.ap.rearrange("p t (g c) -> p t g c", g=8), in_=src, func=AF.Copy),
                     reads=[("ps", pb)], writes=dstv.keys)
                pb2 = ps_next(0, 3)
                for tl in range(8):
                    TR(V(PSB[pb2][:, tl * 128:tl * 128 + nk], [("ps", pb2)]), Z1.view((kb, tl, slice(None)), nk, 0), V(IDNB.full[0:nk, 0:nk], IDNB[:].keys))
                zt = ZT[fc, k0 * 8:k0 * 8 + 8 * nk]
                CP("dve", re(zt, "p (k t) -> p t k", t=8), V(PSB[pb2][:, :].rearrange("p (t k) -> p t k", t=8)[:, :, 0:nk], [("ps", pb2)]))

        DMA("pool", WG[:], wglu_d[j].rearrange("(kc p) n -> p kc n", p=128))
        n_ = 0
        for (t0, n) in tiles:
            who = 2 if t0 < CTX else b
            for oc in range(NFC):
                pa = ps_next(0, 3)
                pg = ps_next(3, 6)
                for kc in range(NFC):
                    MM(psv(pa, 0, n), WG[kc, oc * 128:(oc + 1) * 128], ZT[kc, t0:t0 + n], kc == 0, kc == NFC - 1)
                for kc in range(NFC):
                    MM(psv(pg, 0, n), WG[kc, D + oc * 128:D + (oc + 1) * 128], ZT[kc, t0:t0 + n], kc == 0, kc == NFC - 1)
                sg = SIG[n_ % 2]
                mx = MIX[n_ % 2]
                n_ += 1
                ACT(sg[0:n], psv(pg, 0, n), AF.Sigmoid, bias=vcol(c_bglu(j) + 8 + oc))
                STT("dve", mx[0:n], psv(pa, 0, n), vcol(c_bglu(j) + oc), sg[0:n], ALU.add, ALU.mult)
                xv = XT[oc, t0:t0 + n]
                STT("dve", xv, mx[0:n], mod(i, 2, oc, who), xv, ALU.mult, ALU.add)
        if do_ln1:
            layer_norm(tiles, c_lng(i, 0), c_lnb(i, 0), 0, lnbase)

    s5_consts()
    compute_mods()
    mods_tick(8)
    for jj in sorted(set(i // 2 for (i, dm, _) in plan if dm and i % 2 == 0)):
        s5_tables(jj)
    mods_tick(1000)
    for b in range(nseq):
        load_seq(b)
        for (i, do_mix, do_mlp) in plan:
            tiles = TILES if i < 2 else TILES[1:]
            if do_mix:
                if i % 2 == 0:
                    s5_block(i, b, tiles, do_ln1=not do_mlp)
                else:
                    conv_block(i, b, tiles, do_ln1=not do_mlp)
            if do_mlp:
                mlp_block(i, b, tiles, fuse_ln1=do_mix)
        store_seq(b)

    P.emit()
    print('ops/milestones/waits per engine:', P.stats)
    return nc, es


_CACHE = {}


def host_inputs(inputs):
    f = np.float32
    x = np.asarray(inputs["x"], f)
    ctx = np.asarray(inputs["ctx"], f)
    c = np.asarray(inputs["c"], f)
    c_ctx = np.asarray(inputs["c_ctx"], f)

    def fm(v):
        return np.moveaxis(v.reshape(v.shape[:-1] + (8, 128)), -1, 0)

    rows = []
    rows.append(np.asarray(inputs["b_ada"], f).reshape(4 * 48, 128))
    rows.append(np.asarray(inputs["ln_gain"], f).reshape(64, 128))
    rows.append(np.asarray(inputs["ln_bias"], f).reshape(64, 128))
    rows.append(np.asarray(inputs["s5_d"], f).reshape(16, 128))
    rows.append(np.asarray(inputs["s5_b_glu"], f).reshape(32, 128))
    rows.append(np.asarray(inputs["cv_b_pw1"], f).reshape(32, 128))
    rows.append(np.asarray(inputs["cv_w_dw"], f).reshape(2 * 31 * 8, 128))
    rows.append(np.asarray(inputs["cv_b_dw"], f).reshape(16, 128))
    rows.append(np.asarray(inputs["cv_ln_g"], f).reshape(16, 128))
    rows.append(np.asarray(inputs["cv_ln_b"], f).reshape(16, 128))
    rows.append(np.asarray(inputs["cv_b_pw2"], f).reshape(16, 128))
    vec = np.concatenate(rows, axis=0)
    vec = np.concatenate([vec, np.zeros((NVEC - vec.shape[0], 128), f)], axis=0)
    vecT = np.ascontiguousarray(vec.T)

    def dn(v):
        return np.ascontiguousarray(np.transpose(np.asarray(v, f), (0, 1, 3, 2)).reshape(2, 128, 64))

    shared = {
        "vecT": vecT,
        "w_ada": np.asarray(inputs["w_ada"], f), "mlp_w1": np.asarray(inputs["mlp_w1"], f), "mlp_w2": np.asarray(inputs["mlp_w2"], f),
        "s5_w_glu": np.asarray(inputs["s5_w_glu"], f), "cv_w_pw1": np.asarray(inputs["cv_w_pw1"], f),
        "cv_w_pw2": np.asarray(inputs["cv_w_pw2"], f),
        "lam_reT": dn(inputs["s5_lam_re"]), "lam_imT": dn(inputs["s5_lam_im"]),
        "log_dtT": np.ascontiguousarray(np.broadcast_to(np.asarray(inputs["s5_log_dt"], f)[:, :, None, :], (2, 2, 64, 64)).reshape(2, 128, 64)),
        "b_reT": np.ascontiguousarray(np.transpose(np.asarray(inputs["s5_b_re"], f), (0, 1, 3, 2, 4)).reshape(2, 128, 64, 16)),
        "b_imT": np.ascontiguousarray(np.transpose(np.asarray(inputs["s5_b_im"], f), (0, 1, 3, 2, 4)).reshape(2, 128, 64, 16)),
        "c_reT": np.ascontiguousarray(np.transpose(np.asarray(inputs["s5_c_re"], f), (0, 1, 4, 2, 3)).reshape(2, 128, 64, 16)),
        "c_imT": np.ascontiguousarray(np.transpose(np.asarray(inputs["s5_c_im"], f), (0, 1, 4, 2, 3)).reshape(2, 128, 64, 16)),
        "s5_dT": np.ascontiguousarray(np.tile(np.transpose(np.asarray(inputs["s5_d"], f).reshape(2, 64, 16), (0, 2, 1)), (1, 8, 1))),
    }
    maps = []
    nb = x.shape[0] // 2
    for k in range(nb):
        m = dict(shared)
        m["xT"] = np.ascontiguousarray(np.stack([np.transpose(x[2 * k + s].T.reshape(8, 128, SEQ), (1, 0, 2)) for s in range(2)]))
        m["cT"] = np.ascontiguousarray(np.stack([np.transpose(ctx[2 * k + s].T.reshape(8, 128, CTX), (1, 0, 2)) for s in range(2)]))
        cc = np.stack([c[2 * k], c[2 * k + 1], c_ctx], axis=0)
        m["condT"] = np.ascontiguousarray(np.transpose(cc.reshape(3, 8, 128), (2, 1, 0)))
        maps.append(m)
    return maps


def host_output(res_list):
    outs = []
    for r in res_list:
        yT = r["yT"]
        for s in range(2):
            full = np.transpose(yT[s], (1, 0, 2)).reshape(D, NTOK).T
            outs.append(full[CTX:])
    return np.ascontiguousarray(np.stack(outs, axis=0).astype(np.float32))


def kernel(**inputs):
    if "nc" not in _CACHE:
        _CACHE["nc"] = build_nc()
    nc, es = _CACHE["nc"]
    maps = host_inputs(inputs)
    res = run_bass_kernel_spmd(nc, maps, core_ids=list(range(len(maps))))
    return host_output(res.results)
```

```python
import math
from contextlib import ExitStack
import numpy as np
import concourse.bass as bass
import concourse.mybir as mybir
from concourse.bass_utils import run_bass_kernel_spmd

F32 = mybir.dt.float32
BF16 = mybir.dt.bfloat16
I32 = mybir.dt.int32
AF = mybir.ActivationFunctionType
ALU = mybir.AluOpType

D = 1024
SEQ = 2048
CTX = 256
NTOK = SEQ + CTX
DEPTH = 4
NFC = 8
DFF = 4096
TWO_PI = 2.0 * math.pi
DN_ALPHA = (2.0 * DEPTH) ** 0.25
LN_EPS = 1e-5


import os
NOWAR = os.environ.get('NOWAR', '0') == '1'


class Prog:
    EP = 30000
    KD = 8

    def __init__(self, nc, es):
        self.nc, self.es = nc, es
        self.engs = ["pe", "act", "dve", "pool", "sp"]
        self.q = {e: [] for e in self.engs}
        self.cnt = {e: 0 for e in self.engs}
        self.dma_n = {e: 0 for e in self.engs}
        self.lastw = {}
        self.readers = {}
        self.known = {e: {} for e in self.engs}
        self.knownd = {e: {} for e in self.engs}

    def op(self, eng, fn, reads=(), writes=(), dma=False):
        psr = [k for k in reads if isinstance(k, tuple) and k[0] == "ps"]
        if psr:
            writes = list(writes) + psr
        raw, war = set(), set()
        for k in reads:
            t = self.lastw.get(k)
            if t is not None:
                raw.add(t)
        for k in writes:
            t = self.lastw.get(k)
            if t is not None:
                raw.add(t)
            for t in self.readers.get(k, ()):
                war.add(t)
        if dma:
            n = self.dma_n[eng]
            self.dma_n[eng] += 1
            tok = ("d", eng, n)
            if n >= self.KD:
                raw.add(("d", eng, n - self.KD))
        else:
            tok = ("c", eng, self.cnt[eng])
            self.cnt[eng] += 1
        waits = []
        for kind, deps in (("raw", raw), ("war", war)):
            for t in sorted(deps, reverse=True):
                if t[0] == "c":
                    _, e2, i2 = t
                    if e2 == eng and not dma:
                        if eng == "pe" or (kind == "war" and NOWAR):
                            continue
                    if self.known[eng].get(e2, -1) >= i2:
                        continue
                    self.known[eng][e2] = i2
                    waits.append(t)
                else:
                    _, q2, n2 = t
                    slot = (q2, n2 % self.KD)
                    if self.knownd[eng].get(slot, -1) >= n2:
                        continue
                    self.knownd[eng][slot] = n2
                    waits.append(t)
        for k in reads:
            self.readers.setdefault(k, []).append(tok)
        for k in writes:
            self.lastw[k] = tok
            self.readers[k] = []
        self.q[eng].append((fn, waits, tok))
        return tok

    def emit(self):
        nc, es = self.nc, self.es
        mil = {e: set() for e in self.engs}
        for e in self.engs:
            for fn, waits, tok in self.q[e]:
                for t in waits:
                    if t[0] == "c":
                        mil[t[1]].add(t[2])
        rank = {}
        nm = {}
        for e in self.engs:
            srt = sorted(mil[e])
            nm[e] = len(srt)
            for r, idx in enumerate(srt):
                rank[(e, idx)] = r
        psem = {e: [es.enter_context(nc.semaphore(f"p_{e}_{i}")) for i in range(nm[e] // self.EP + 1)]
                for e in self.engs}
        dsem = {e: [es.enter_context(nc.semaphore(f"d_{e}_{i}")) for i in range(self.KD)]
                for e in self.engs if self.dma_n[e] > 0}
        block = es.enter_context(nc.Block())
        self.stats = {e: (len(self.q[e]), nm[e], sum(len(w) for _, w, _ in self.q[e])) for e in self.engs}

        def resolve(t):
            if t[0] == "c":
                r = rank[(t[1], t[2])]
                return psem[t[1]][r // self.EP], (r % self.EP) + 1
            _, q2, n2 = t
            return dsem[q2][n2 % self.KD], 16 * (n2 // self.KD + 1)

        def run(e, eng):
            for fn, waits, tok in self.q[e]:
                for t in waits:
                    s, v = resolve(t)
                    eng.wait_ge(s, v)
                inst = fn(eng)
                if tok[0] == "c":
                    r = rank.get((e, tok[2]))
                    if r is not None:
                        inst.then_inc(psem[e][r // self.EP], 1)
                else:
                    inst.then_inc(dsem[e][tok[2] % self.KD], 16)
            n = self.dma_n[e]
            for j in range(max(0, n - self.KD), n):
                s, v = resolve(("d", e, j))
                eng.wait_ge(s, v)

        @block.tensor
        def _(eng):
            run("pe", eng)

        @block.scalar
        def _(eng):
            run("act", eng)

        @block.vector
        def _(eng):
            run("dve", eng)

        @block.gpsimd
        def _(eng):
            run("pool", eng)

        @block.sync
        def _(eng):
            run("sp", eng)


BLK = 512
AX = mybir.AxisListType

def c_bada(i): return i * 48
def c_lng(i, k): return 192 + (i * 2 + k) * 8
def c_lnb(i, k): return 256 + (i * 2 + k) * 8
def c_s5d(j): return 320 + j * 8
def c_bglu(j): return 336 + j * 16
def c_bpw1(j): return 368 + j * 16
def c_wdw(j, tap): return 400 + (j * 31 + tap) * 8
def c_bdw(j): return 896 + j * 8
def c_cvg(j): return 912 + j * 8
def c_cvb(j): return 928 + j * 8
def c_bpw2(j): return 944 + j * 8
NVEC = 1024


class V:
    __slots__ = ("ap", "keys")

    def __init__(self, ap, keys):
        self.ap, self.keys = ap, keys


class Buf:
    def __init__(self, AR, off, shape, dt):
        self.off, self.shape, self.dt = off, list(shape), dt
        self.esz = 2 if dt == BF16 else 4
        n = 1
        for d in shape:
            n *= d
        self.nbytes = n * self.esz
        assert off % 4 == 0 and self.nbytes % 4 == 0
        ap = AR[:, off // 4:(off + self.nbytes) // 4]
        if dt != F32:
            ap = ap.bitcast(dt)
        if len(shape) == 2:
            ap = ap.rearrange("p (a b) -> p a b", a=shape[0])
        elif len(shape) == 3:
            ap = ap.rearrange("p (a b c) -> p a b c", a=shape[0], b=shape[1])
        elif len(shape) == 4:
            ap = ap.rearrange("p (a b c d) -> p a b c d", a=shape[0], b=shape[1], c=shape[2])
        self.full = ap
        self.strides = []
        st = 1
        for d in reversed(shape):
            self.strides.insert(0, st)
            st *= d

    def _rng(self, idx, dims):
        lo = hi = 0
        for k, d in zip(idx, dims):
            st = self.strides[d]
            if isinstance(k, int):
                lo += k * st
                hi += k * st
            else:
                a, b, c = k.indices(self.shape[d])
                last = a + ((b - a - 1) // c) * c
                lo += a * st
                hi += last * st
        return lo, hi

    def __getitem__(self, idx):
        if not isinstance(idx, tuple):
            idx = (idx,)
        idx = tuple(idx) + (slice(None),) * (len(self.shape) - len(idx))
        return self.view(idx, 128)

    def view(self, idx, nparts=128, p0=0):
        ap = self.full[(slice(p0, p0 + nparts),) + tuple(idx)]
        keys = set()
        first = idx[0]
        if isinstance(first, int):
            firsts = [first]
        else:
            a, b, c = first.indices(self.shape[0])
            firsts = list(range(a, b, c))
        for f in firsts:
            lo, hi = self._rng((f,) + tuple(idx[1:]), range(len(self.shape)))
            b0 = (self.off + lo * self.esz) // BLK
            b1 = (self.off + (hi + 1) * self.esz - 1) // BLK
            for bb in range(b0, b1 + 1):
                keys.add(bb)
        return V(ap, list(keys))


def build_nc(plan=None, nseq=2, pos=True, dbg=False):
    if plan is None:
        plan = [(i, True, True) for i in range(DEPTH)]
    nc = bass.Bass("TRN2", target_bir_lowering=False)
    es = ExitStack()
    P = Prog(nc, es)

    def din(name, shape):
        return nc.dram_tensor(name, list(shape), F32, kind="ExternalInput").ap()

    xT_d = din("xT", [2, 128, NFC, SEQ])
    cT_d = din("cT", [2, 128, NFC, CTX])
    cond_d = din("condT", [128, NFC, 3])
    vec_d = din("vecT", [128, NVEC])
    wada_d = din("w_ada", [DEPTH, D, 6 * D])
    w1_d = din("mlp_w1", [DEPTH, D, DFF])
    w2_d = din("mlp_w2", [DEPTH, DFF, D])
    wglu_d = din("s5_w_glu", [2, D, 2 * D])
    wpw1_d = din("cv_w_pw1", [2, D, 2 * D])
    wpw2_d = din("cv_w_pw2", [2, D, D])
    lamre_d = din("lam_reT", [2, 128, 64])
    lamim_d = din("lam_imT", [2, 128, 64])
    logdt_d = din("log_dtT", [2, 128, 64])
    bre_d = din("b_reT", [2, 128, 64, 16])
    bim_d = din("b_imT", [2, 128, 64, 16])
    cre_d = din("c_reT", [2, 128, 64, 16])
    cim_d = din("c_imT", [2, 128, 64, 16])
    s5d_d = din("s5_dT", [2, 128, 64])
    y_d = nc.dram_tensor("yT", [2, 128, NFC, NTOK], F32, kind="ExternalOutput").ap()
    dbg_d = nc.dram_tensor("dbg", [128, 16384], F32, kind="ExternalOutput").ap() if dbg else None

    ARBYTES = 212480
    AR = es.enter_context(nc.sbuf_tensor("AR", [128, ARBYTES // 4], F32))
    PS = [es.enter_context(nc.psum_tensor(f"ps{i}", [128, 512], F32)) for i in range(8)]
    PSB = [p[:, :].bitcast(BF16) for p in PS]
    psrr = {}

    def ps_next(lo, hi):
        i = psrr.get((lo, hi), 0)
        psrr[(lo, hi)] = i + 1
        return lo + i % (hi - lo)

    def psv(i, a=0, b=512):
        return V(PS[i][:, a:b], [("ps", i)])

    def psvb(i, a=0, b=1024):
        return V(PSB[i][:, a:b], [("ps", i)])

    top = {"o": 0}

    def alloc(shape, dt=F32, at=None):
        off = top["o"] if at is None else at
        bf = Buf(AR, off, shape, dt)
        end = off + ((bf.nbytes + BLK - 1) // BLK) * BLK
        if at is None:
            top["o"] = end
        assert end <= ARBYTES, (end, ARBYTES)
        return bf

    def _k(*vs):
        out = []
        for v in vs:
            if isinstance(v, V):
                out += v.keys
        return out

    def _a(v):
        return v.ap if isinstance(v, V) else v

    def MM(out, lhsT, rhs, start, stop):
        P.op("pe", lambda e: e.matmul(out.ap, lhsT.ap, rhs.ap, start=start, stop=stop),
             reads=_k(lhsT, rhs), writes=_k(out))

    def TR(out, in_, idn):
        P.op("pe", lambda e: e.transpose(out=out.ap, in_=in_.ap, identity=idn.ap), reads=_k(in_, idn), writes=_k(out))

    def ACT(out, in_, func, scale=None, bias=None):
        kw = {}
        if scale is not None:
            kw["scale"] = _a(scale)
        if bias is not None:
            kw["bias"] = _a(bias)
        P.op("act", lambda e: e.activation(out=out.ap, in_=in_.ap, func=func, **kw),
             reads=_k(in_, scale, bias), writes=_k(out))

    def TT(eng, out, in0, in1, op):
        P.op(eng, lambda e: e.tensor_tensor(out=out.ap, in0=in0.ap, in1=in1.ap, op=op), reads=_k(in0, in1), writes=_k(out))

    def TS(eng, out, in0, s1, s2, op0, op1=None):
        if op1 is None:
            P.op(eng, lambda e: e.tensor_scalar(out=out.ap, in0=in0.ap, scalar1=_a(s1), scalar2=None, op0=op0),
                 reads=_k(in0, s1), writes=_k(out))
        else:
            P.op(eng, lambda e: e.tensor_scalar(out=out.ap, in0=in0.ap, scalar1=_a(s1), scalar2=_a(s2), op0=op0, op1=op1),
                 reads=_k(in0, s1, s2), writes=_k(out))

    def STT(eng, out, in0, scalar, in1, op0, op1):
        P.op(eng, lambda e: e.scalar_tensor_tensor(out=out.ap, in0=in0.ap, scalar=_a(scalar), in1=in1.ap, op0=op0, op1=op1),
             reads=_k(in0, scalar, in1), writes=_k(out))

    def CP(eng, out, in_):
        if eng == "act":
            P.op(eng, lambda e: e.activation(out=out.ap, in_=in_.ap, func=AF.Copy), reads=_k(in_), writes=_k(out))
        else:
            P.op(eng, lambda e: e.tensor_copy(out=out.ap, in_=in_.ap), reads=_k(in_), writes=_k(out))

    def RED(eng, out, in_, op=ALU.add):
        P.op(eng, lambda e: e.tensor_reduce(out=out.ap, in_=in_.ap, axis=AX.X, op=op), reads=_k(in_), writes=_k(out))

    def RECIP(out, in_):
        P.op("dve", lambda e: e.reciprocal(out=out.ap, in_=in_.ap), reads=_k(in_), writes=_k(out))

    def MEMSET(eng, out, val):
        P.op(eng, lambda e: e.memset(out.ap, val), writes=_k(out))

    def IOTA(out, pattern, base, cm):
        P.op("pool", lambda e: e.iota(out.ap, pattern, base=base, channel_multiplier=cm), writes=_k(out))

    def DMA(queue, out, in_):
        P.op(queue, lambda e: e.dma_start(out=_a(out), in_=_a(in_)), reads=_k(in_), writes=_k(out), dma=True)

    def DUMP(v, c0, n):
        if dbg:
            DMA("sp", dbg_d[:, c0:c0 + n], v)

    def bc(v, shape):
        return V(v.ap.to_broadcast(list(shape)), v.keys)

    def re(v, pattern, **kw):
        return V(v.ap.rearrange(pattern, **kw), v.keys)

    XT = alloc([NFC, NTOK])
    VEC = alloc([NVEC])
    MOD = alloc([DEPTH, 6, NFC, 3])
    IDN = alloc([128])
    IDNB = alloc([128], BF16)
    ONES = alloc([128])
    CST = alloc([8])
    OM = alloc([2])
    EXPT = alloc([3, 5, 8])
    MSKF = alloc([128])
    AR2P = alloc([2, 64, 2])
    AI2P = alloc([2, 64, 2])
    MSKB = alloc([128])
    ARENA0 = top["o"]

    TILES = [(0, 256)] + [(256 + 512 * i, 512) for i in range(4)]

    def vcol(c):
        return VEC[c:c + 1]

    tI = alloc([128], I32, at=ARENA0)
    IOTA(tI[:], [[1, 128]], 0, -1)
    CP("dve", IDN[:], tI[:])
    P.op("dve", lambda e: e.tensor_single_scalar(out=IDN[:].ap, in_=IDN[:].ap, scalar=0.0, op=ALU.is_equal),
         reads=IDN[:].keys, writes=IDN[:].keys)
    CP("dve", IDNB[:], IDN[:])
    MEMSET("dve", ONES[:], 1.0)
    MEMSET("dve", CST[0:1], LN_EPS / (DN_ALPHA * DN_ALPHA))
    MEMSET("dve", CST[1:2], LN_EPS)
    MEMSET("dve", CST[2:3], 1.0)
    tI2 = alloc([2], I32, at=ARENA0 + 1024)
    IOTA(tI2[:], [[128, 2]], 0, 1)
    CP("dve", OM[:], tI2[:])
    ACT(OM[:], OM[:], AF.Exp, scale=-math.log(10000.0) / 256.0)
    DMA("sp", VEC[:], vec_d[:, :])

    def sin_reduce(a, qf, qi, eng="dve"):
        TS(eng, qf, a, 1.0 / TWO_PI, None, ALU.mult)
        CP(eng, qi, qf)
        CP(eng, qf, qi)
        STT(eng, a, qf, -TWO_PI, a, ALU.mult, ALU.add)
        TS(eng, qf, a, 0.0, None, ALU.is_lt)
        STT(eng, a, qf, TWO_PI, a, ALU.mult, ALU.add)
        TS(eng, a, a, -math.pi, math.pi, ALU.add, ALU.min)

    MODS_BASE = ARENA0 + 112640
    mods_steps = []

    def compute_mods():
        o = MODS_BASE
        CF = alloc([NFC, 3], F32, at=o)
        CB = alloc([NFC, 3], BF16, at=o + 512)
        WAD = [alloc([NFC, 768], BF16, at=o + 1024)]
        DMA("sp", CF[:], cond_d[:, :, :])
        ACT(CB[:], CF[:], AF.Silu)
        state = {"n": 0, "pb": None}

        def piece_step(i, piece):
            def run():
                if piece == 0:
                    state["pb"] = ps_next(6, 8)
                pb = state["pb"]
                w = WAD[0]
                state["n"] += 1
                DMA("pool", w[:], wada_d[i].rearrange("(kc p) n -> p kc n", p=128)[:, :, piece * 768:(piece + 1) * 768])
                for o6 in range(6):
                    oc = piece * 6 + o6
                    for kc in range(NFC):
                        MM(psv(pb, oc * 3, oc * 3 + 3), w[kc, o6 * 128:(o6 + 1) * 128], CB[kc, :], kc == 0, kc == NFC - 1)
                if piece == 7:
                    mo = re(MOD[i], "p m f w -> p (m f) w")
                    TT("dve", mo, re(psv(pb, 0, 144), "p (o w) -> p o w", w=3),
                       bc(re(VEC[c_bada(i):c_bada(i) + 48], "p (o w) -> p o w", w=1), [128, 48, 3]), ALU.add)
                    for m in (1, 4):
                        TS("dve", MOD[i, m], MOD[i, m], 1.0, None, ALU.add)
                    for m in (2, 5):
                        TS("dve", MOD[i, m], MOD[i, m], 1.0 / DN_ALPHA, None, ALU.mult)
                    if i == DEPTH - 1:
                        DUMP(re(MOD[:], "p a b c d -> p (a b c d)"), 0, 576)
            return run
        for i in range(DEPTH):
            for piece in range(8):
                mods_steps.append(piece_step(i, piece))

    def mods_tick(n):
        for _ in range(n):
            if mods_steps:
                mods_steps.pop(0)()

    def mod(i, m, fc, who):
        return MOD[i, m, fc, who:who + 1]

    def load_seq(b):
        for fc in range(NFC):
            DMA("sp", XT[fc, 0:CTX], cT_d[b, :, fc, :])
            DMA("sp", XT[fc, CTX:NTOK], xT_d[b, :, fc, :])
        if not pos:
            return
        o = ARENA0
        PEt = alloc([512], F32, at=o)
        posr = alloc([512], F32, at=o + 2048)
        posc = alloc([512], F32, at=o + 4096)
        posi = alloc([512], I32, at=o + 6144)
        ang = alloc([512], F32, at=o + 8192)
        angq = alloc([512], F32, at=o + 10240)
        angi = alloc([512], I32, at=o + 12288)
        IOTA(posi[:], [[0, 8], [1, 64]], 0, 0)
        CP("dve", posc[:], posi[:])
        for t in range(4):
            IOTA(posi[:], [[1, 8], [0, 64]], t * 8, 0)
            CP("dve", posr[:], posi[:])
            for fc in range(NFC):
                src = posr if fc < 4 else posc
                shift = math.pi + (math.pi / 2 if (fc // 2) % 2 == 1 else 0.0)
                j = fc % 2
                TS("dve", ang[:], src[:], OM[j:j + 1], shift, ALU.mult, ALU.add)
                sin_reduce(ang[:], angq[:], angi[:])
                ACT(PEt[:], ang[:], AF.Sin)
                xs = XT[fc, CTX + t * 512:CTX + (t + 1) * 512]
                TT("dve", xs, xs, PEt[:], ALU.add)

    def store_seq(b):
        for fc in range(NFC):
            DMA("sp", y_d[b, :, fc, :], XT[fc, :])

    def layer_norm(tiles, gcol, bcol, eps_idx, base, src=None, dst=None, func=AF.Identity):
        src = src or (lambda fc, t0, n: XT[fc, t0:t0 + n])
        srcall = None
        SQ = alloc([NFC, 512], F32, at=base)
        S1 = alloc([512], F32, at=base + 16384)
        S2 = alloc([512], F32, at=base + 18432)
        MEAN = alloc([512], F32, at=base + 20480)
        RSTD = alloc([512], F32, at=base + 22528)
        for (t0, n) in tiles:
            for fc in range(NFC):
                ACT(SQ[fc, 0:n], src(fc, t0, n), AF.Square)
            xs = [src(fc, t0, n) for fc in range(NFC)]
            TT("pool", S1[0:n], xs[0], xs[1], ALU.add)
            for fc in range(2, NFC):
                TT("pool", S1[0:n], S1[0:n], xs[fc], ALU.add)
            RED("dve", S2[0:n], re(SQ[:, 0:n], "p f n -> p n f"))
            p1 = ps_next(6, 8)
            p2 = ps_next(6, 8)
            MM(psv(p1, 0, n), ONES[:], S1[0:n], True, True)
            MM(psv(p2, 0, n), ONES[:], S2[0:n], True, True)
            TS("dve", MEAN[0:n], psv(p1, 0, n), 1.0 / D, None, ALU.mult)
            TT("dve", S1[0:n], MEAN[0:n], MEAN[0:n], ALU.mult)
            STT("dve", S2[0:n], psv(p2, 0, n), 1.0 / D, S1[0:n], ALU.mult, ALU.subtract)
            ACT(S2[0:n], S2[0:n], AF.Sqrt, bias=CST[eps_idx:eps_idx + 1])
            RECIP(RSTD[0:n], S2[0:n])
            for fc in range(NFC):
                xv = src(fc, t0, n)
                ne = "dve"
                TT(ne, SQ[fc, 0:n], xv, MEAN[0:n], ALU.subtract)
                TT(ne, SQ[fc, 0:n], SQ[fc, 0:n], RSTD[0:n], ALU.mult)
                ov = dst(fc, t0, n) if dst else xv
                ACT(ov, SQ[fc, 0:n], func, scale=vcol(gcol + fc), bias=vcol(bcol + fc))

    def mlp_block(i, b, tiles, fuse_ln1=False):
        o = ARENA0
        HM = alloc([NFC, NTOK], BF16, at=o); o += 36864
        W1S = [alloc([NFC, 512], BF16, at=o + s * 8192) for s in range(2)]; o += 16384
        W2S = [alloc([4, D], BF16, at=o + s * 8192) for s in range(2)]; o += 16384
        HID = [alloc([4, 512], BF16, at=o + s * 4096) for s in range(2)]; o += 8192
        RL = [alloc([512], F32, at=o + s * 2048) for s in range(2)]; o += 4096
        EV = [alloc([512], F32, at=o + s * 2048) for s in range(2)]; o += 4096
        lnbase = o
        assert lnbase + 24576 <= ARBYTES

        def load_w(q):
            s = q % 2
            DMA("pool", W1S[s][:], w1_d[i].rearrange("(kc p) n -> p kc n", p=128)[:, :, q * 512:(q + 1) * 512])
            DMA("pool", W2S[s][:], w2_d[i, q * 512:(q + 1) * 512, :].rearrange("(jj p) n -> p jj n", p=128))

        def prep(t0, n):
            who = 2 if t0 < CTX else b
            if fuse_ln1:
                layer_norm([(t0, n)], c_lng(i, 0), c_lnb(i, 0), 0, lnbase)
            for fc in range(NFC):
                ACT(HM[fc, t0:t0 + n], XT[fc, t0:t0 + n], AF.Identity, scale=mod(i, 4, fc, who), bias=mod(i, 3, fc, who))

        load_w(0)
        work = [(q, t0, n) for q in range(8) for (t0, n) in tiles]
        nt = len(tiles)
        rl_cnt = [0]

        def Hstage(k):
            q, t0, n = work[k]
            s = q % 2
            hs = k % 2
            for jj in range(4):
                pb = ps_next(0, 3)
                for kc in range(NFC):
                    MM(psv(pb, 0, n), W1S[s][kc, jj * 128:(jj + 1) * 128], HM[kc, t0:t0 + n], kc == 0, kc == NFC - 1)
                r = RL[rl_cnt[0] % 2]
                rl_cnt[0] += 1
                ACT(r[0:n], psv(pb, 0, n), AF.Relu)
                TT("pool" if jj % 2 else "dve", HID[hs][jj, 0:n], r[0:n], r[0:n], ALU.mult)

        def Ostage(k):
            q, t0, n = work[k]
            s = q % 2
            who = 2 if t0 < CTX else b
            hs = k % 2
            for f in range(NFC):
                pb = ps_next(3, 6)
                for jj in range(4):
                    MM(psv(pb, 0, n), W2S[s][jj, f * 128:(f + 1) * 128], HID[hs][jj, 0:n], jj == 0, jj == 3)
                xv = XT[f, t0:t0 + n]
                if f % 2 == 0:
                    STT("dve", xv, psv(pb, 0, n), mod(i, 5, f, who), xv, ALU.mult, ALU.add)
                else:
                    ev = EV[(f // 2) % 2]
                    ACT(ev[0:n], psv(pb, 0, n), AF.Identity, scale=mod(i, 5, f, who))
                    TT("pool", xv, xv, ev[0:n], ALU.add)

        prep(*tiles[0])
        if nt > 1:
            prep(*tiles[1])
        for k in range(len(work) + 1):
            if k < len(work):
                q, t0, n = work[k]
                ti = k % nt
                if q == 0 and ti + 2 < nt:
                    prep(*tiles[ti + 2])
                Hstage(k)
            if k >= 1:
                Ostage(k - 1)
                q1, t01, n1 = work[k - 1]
                if (k - 1) % nt == 0 and q1 + 1 < 8:
                    load_w(q1 + 1)
                if q1 == 7:
                    layer_norm([(t01, n1)], c_lng(i, 1), c_lnb(i, 1), 0, lnbase)

    def acol(tok):
        return tok + 15 if tok < CTX else tok + 45

    def conv_block(i, b, tiles, do_ln1=True):
        j = i // 2
        o = ARENA0
        NA = 2364
        AB = alloc([NFC, NA], BF16, at=o)
        oA = o + 38912
        WP1 = alloc([NFC, 2 * D], BF16, at=oA)
        HMt = [alloc([NFC, 512], BF16, at=oA + 32768 + s * 8192) for s in range(2)]
        SIG = [alloc([512], F32, at=oA + 49152 + s * 2048) for s in range(2)]
        WP2 = alloc([NFC, D], BF16, at=oA + 69632)
        DG = [alloc([31, 128], BF16, at=oA + s * 8192) for s in range(2)]
        CO = alloc([NFC, 512], F32, at=oA + 16384)
        NB = alloc([NFC, 512], BF16, at=oA + 32768)
        TMP = [alloc([512], F32, at=oA + 40960 + s * 2048) for s in range(2)]
        lnbase = oA + 45056
        DMA("pool", WP1[:], wpw1_d[j].rearrange("(kc p) n -> p kc n", p=128))
        DMA("pool", WP2[:], wpw2_d[j].rearrange("(kc p) n -> p kc n", p=128))
        MEMSET("dve", AB[:, 0:15], 0.0)
        MEMSET("dve", AB[:, 271:301], 0.0)
        MEMSET("dve", AB[:, 2349:2364], 0.0)
        n_ = 0
        for (t0, n) in tiles:
            who = 2 if t0 < CTX else b
            hm = HMt[n_ % 2]
            for fc in range(NFC):
                ACT(hm[fc, 0:n], XT[fc, t0:t0 + n], AF.Identity, scale=mod(i, 1, fc, who), bias=mod(i, 0, fc, who))
            for oc in range(NFC):
                pa = ps_next(0, 3)
                pg = ps_next(3, 6)
                for kc in range(NFC):
                    MM(psv(pa, 0, n), WP1[kc, oc * 128:(oc + 1) * 128], hm[kc, 0:n], kc == 0, kc == NFC - 1)
                for kc in range(NFC):
                    MM(psv(pg, 0, n), WP1[kc, D + oc * 128:D + (oc + 1) * 128], hm[kc, 0:n], kc == 0, kc == NFC - 1)
                sg = SIG[n_ % 2]
                n_ += 1
                ACT(sg[0:n], psv(pg, 0, n), AF.Sigmoid, bias=vcol(c_bpw1(j) + 8 + oc))
                c0 = acol(t0)
                STT("dve", AB[oc, c0:c0 + n], psv(pa, 0, n), vcol(c_bpw1(j) + oc), sg[0:n], ALU.add, ALU.mult)
        m_ = 0
        for (t0, n) in tiles:
            who = 2 if t0 < CTX else b
            c0 = acol(t0)
            for fc in range(NFC):
                dg = DG[m_ % 2]
                m_ += 1
                wv = V(VEC.full[:, c_wdw(j, 0) + fc:c_wdw(j, 31) + fc:8], VEC[c_wdw(j, 0):c_wdw(j, 31)].keys)
                TT("dve", dg[:], bc(re(IDNB[:], "p (o x) -> p o x", o=1), [128, 31, 128]),
                   bc(re(wv, "p (t o) -> p t o", o=1), [128, 31, 128]), ALU.mult)
                pb = ps_next(0, 3)
                for tap in range(31):
                    MM(psv(pb, 0, n), dg[tap, :], AB[fc, c0 + tap - 15:c0 + tap - 15 + n], tap == 0, tap == 30)
                ACT(CO[fc, 0:n], psv(pb, 0, n), AF.Identity, bias=vcol(c_bdw(j) + fc))
            layer_norm([(0, n)], c_cvg(j), c_cvb(j), 1, lnbase,
                       src=lambda fc, t0_, n_: CO[fc, 0:n_], dst=lambda fc, t0_, n_: NB[fc, 0:n_], func=AF.Silu)
            for f in range(NFC):
                pb = ps_next(3, 6)
                for kc in range(NFC):
                    MM(psv(pb, 0, n), WP2[kc, f * 128:(f + 1) * 128], NB[kc, 0:n], kc == 0, kc == NFC - 1)
                tm = TMP[f % 2]
                ACT(tm[0:n], psv(pb, 0, n), AF.Identity, bias=vcol(c_bpw2(j) + f))
                xv = XT[f, t0:t0 + n]
                STT("dve", xv, tm[0:n], mod(i, 2, f, who), xv, ALU.mult, ALU.add)
        if do_ln1:
            layer_norm(tiles, c_lng(i, 0), c_lnb(i, 0), 0, lnbase)

    NH = 4
    KCH = NTOK // 32
    NKAP = NTOK // 8
    KOFF = 23
    KBS = [(0, 128), (128, 128), (256, 32)]

    def s5_consts():
        ti = alloc([3, 5, 8], I32, at=ARENA0 + 2048)
        MEMSET("dve", ti[:], 0)
        P.op("pool", lambda e: e.iota(ti.view((0, slice(None), slice(None)), 64, 0).ap, [[-8, 5], [-1, 8]], base=32, channel_multiplier=0),
             writes=ti[:].keys)
        P.op("pool", lambda e: e.iota(ti.view((0, slice(None), slice(None)), 64, 64).ap, [[8, 5], [1, 8]], base=0, channel_multiplier=0),
             writes=ti[:].keys)
        P.op("pool", lambda e: e.iota(ti.view((1, slice(None), slice(None)), 64, 0).ap, [[8, 5], [1, 8]], base=0, channel_multiplier=0),
             writes=ti[:].keys)
        P.op("pool", lambda e: e.iota(ti.view((1, slice(None), slice(None)), 64, 64).ap, [[-8, 5], [-1, 8]], base=32, channel_multiplier=0),
             writes=ti[:].keys)
        P.op("pool", lambda e: e.iota(ti.view((2, 0, slice(None)), 64, 0).ap, [[-1, 8]], base=0, channel_multiplier=0),
             writes=ti[:].keys)
        P.op("pool", lambda e: e.iota(ti.view((2, 0, slice(None)), 64, 64).ap, [[1, 8]], base=0, channel_multiplier=0),
             writes=ti[:].keys)
        CP("dve", EXPT[:], ti[:])
        mi = alloc([128], I32, at=ARENA0 + 4096)
        IOTA(re(mi[:], "p (t c) -> p t c", c=16), [[16, 8], [0, 16]], 0, -1)
        CP("dve", MSKF[:], mi[:])
        CP("dve", MSKB[:], mi[:])
        TS("dve", MSKF[:], MSKF[:], -15.0, None, ALU.is_ge)
        TS("dve", MSKB[:], MSKB[:], 0.0, None, ALU.is_le)

    def pw_tables(dst_r, dst_i, lrdt_v, lidt_v, ex_v, shp, tb):
        G, M = shp
        ANG, A2, QF, QI = tb
        lr_b = bc(re(lrdt_v, "p (g o) -> p g o", o=1), [128, G, M])
        li_b = bc(re(lidt_v, "p (g o) -> p g o", o=1), [128, G, M])
        ex_b = bc(re(ex_v, "p (o m) -> p o m", o=1), [128, G, M])
        TT("dve", ANG[:], li_b, ex_b, ALU.mult)
        TT("dve", A2[:], lr_b, ex_b, ALU.mult)
        ACT(dst_i[:], A2[:], AF.Exp)
        TS("dve", A2[:], ANG[:], math.pi + TWO_PI * KOFF + math.pi / 2, None, ALU.add)
        sin_reduce(A2[:], QF[:], QI[:])
        ACT(A2[:], A2[:], AF.Sin)
        TT("dve", dst_r[:], dst_i[:], A2[:], ALU.mult)
        TS("dve", A2[:], ANG[:], math.pi + TWO_PI * KOFF, None, ALU.add)
        sin_reduce(A2[:], QF[:], QI[:])
        ACT(A2[:], A2[:], AF.Sin)
        TT("dve", dst_i[:], dst_i[:], A2[:], ALU.mult)

    def cmul_bc(dr, di, ar, ai, br, bi, tmp, neg_im=False):
        TT("dve", dr, ar, br, ALU.mult)
        TT("dve", tmp, ai, bi, ALU.mult)
        TT("dve", dr, dr, tmp, ALU.subtract)
        TT("dve", di, ar, bi, ALU.mult)
        TT("dve", tmp, ai, br, ALU.mult)
        TT("dve", di, di, tmp, ALU.add)

    WIN_D = [nc.dram_tensor(f"win_s{j}", [128, 64, 2, 4, 128], BF16).ap() for j in range(2)]
    WOUT_D = [nc.dram_tensor(f"wout_s{j}", [128, 64, 2, 4, 128], BF16).ap() for j in range(2)]
    IM_D = [nc.dram_tensor(f"im_s{j}", [128, 64, 7, 128], BF16).ap() for j in range(2)]

    def bcg(v, n_in):
        return bc(re(v, "p (g o) -> p g o", o=1), [128, v.ap.shape[1], n_in])

    def cmul_e(eng, dr, di, ar, ai, br, bi, tmp):
        TT(eng, dr, ar, br, ALU.mult)
        TT(eng, tmp, ai, bi, ALU.mult)
        TT(eng, dr, dr, tmp, ALU.subtract)
        TT(eng, di, ar, bi, ALU.mult)
        TT(eng, tmp, ai, br, ALU.mult)
        TT(eng, di, di, tmp, ALU.add)

    def s5_tables(j):
        o = ARENA0
        G8 = 8
        M = 40
        TAB = alloc([9, 64], F32, at=o); o += 2560
        LRDT, LIDT, COR, COI, T1, T2, T3, T4, DST = range(9)
        PQ = alloc([64, 2], F32, at=o); o += 512
        PQ2 = alloc([64, 2], F32, at=o); o += 512
        EX1 = alloc([2], F32, at=o); o += 512
        MA = 88
        SL = [o + k * 3072 for k in range(6)]
        o += 6 * 3072
        PWt = [alloc([G8, MA], F32, at=SL[k]) for k in range(3)]
        PWi = alloc([G8, MA], I32, at=SL[3])
        Pr = alloc([G8, MA], F32, at=SL[4])
        Pi = alloc([G8, MA], F32, at=SL[5])
        Er = alloc([G8, MA], F32, at=SL[0])
        Ei = alloc([G8, MA], F32, at=SL[1])
        ETMP = alloc([G8, MA], F32, at=SL[2])
        Pin = alloc([G8, M], F32, at=o); o += 1536
        BR = alloc([G8, 16], F32, at=o)
        BI = alloc([G8, 16], F32, at=o + 512)
        CR = alloc([G8, 16], F32, at=o + 1024)
        CI = alloc([G8, 16], F32, at=o + 1536)
        o += 2048
        D1 = alloc([128], F32, at=o)
        D2 = alloc([128], F32, at=o + 512)
        o += 1024
        G4 = 4
        LRE = alloc([G4, 5, 128], F32, at=o); o += 10240
        LIM = alloc([G4, 5, 128], F32, at=o); o += 10240
        LT = alloc([G4, 5, 128], F32, at=o); o += 10240
        LT2 = alloc([G4, 5, 128], F32, at=o); o += 10240
        L0R = alloc([G8, 128], F32, at=o); o += 4096
        L0I = alloc([G8, 128], F32, at=o); o += 4096
        L0T = alloc([G8, 128], F32, at=o); o += 4096
        WIN = alloc([G8, 2, 4, 128], BF16, at=o); o += 16384
        IM = alloc([G8, 7, 128], BF16, at=o); o += 14336
        assert o <= MODS_BASE, (o, MODS_BASE)

        DMA("sp", TAB[T1], lamre_d[j, :, :])
        DMA("sp", TAB[T2], lamim_d[j, :, :])
        DMA("sp", TAB[T3], logdt_d[j, :, :])
        DMA("sp", TAB[DST], s5d_d[j, :, :])
        TS("dve", TAB[T1], TAB[T1], -1e-4, None, ALU.min)
        ACT(TAB[T3], TAB[T3], AF.Exp)
        TT("dve", TAB[LRDT], TAB[T1], TAB[T3], ALU.mult)
        TT("dve", TAB[LIDT], TAB[T2], TAB[T3], ALU.mult)
        MEMSET("dve", EX1[0:1], 1.0)
        MEMSET("dve", EX1[1:2], 32.0)
        tb = [alloc([64, 2], F32, at=LRE.off + k * 512) for k in range(3)] + [alloc([64, 2], I32, at=LRE.off + 1536)]
        pw_tables(PQ, PQ2, TAB[LRDT], TAB[LIDT], EX1[:], (64, 2), tb)
        a1r = re(PQ[:, 0:1], "p g o -> p (g o)")
        a1i = re(PQ2[:, 0:1], "p g o -> p (g o)")
        TS("dve", TAB[T4], a1r, -1.0, None, ALU.add)
        TT("dve", TAB[COR], TAB[T4], TAB[T1], ALU.mult)
        TT("dve", TAB[COI], a1i, TAB[T2], ALU.mult)
        TT("dve", TAB[COR], TAB[COR], TAB[COI], ALU.add)
        TT("dve", TAB[COI], a1i, TAB[T1], ALU.mult)
        TT("dve", TAB[T4], TAB[T4], TAB[T2], ALU.mult)
        TT("dve", TAB[COI], TAB[COI], TAB[T4], ALU.subtract)
        TT("dve", TAB[T1], TAB[T1], TAB[T1], ALU.mult)
        TT("dve", TAB[T2], TAB[T2], TAB[T2], ALU.mult)
        TT("dve", TAB[T1], TAB[T1], TAB[T2], ALU.add)
        RECIP(TAB[T1], TAB[T1])
        TT("dve", TAB[COR], TAB[COR], TAB[T1], ALU.mult)
        TT("dve", TAB[COI], TAB[COI], TAB[T1], ALU.mult)
        aTr = PQ[:, 1:2]
        aTi = PQ2[:, 1:2]
        CP("dve", AR2P[j, :, 0:1], aTr)
        CP("dve", AR2P[j, :, 1:2], aTr)
        CP("dve", AI2P[j, :, 1:2], aTi)
        TS("dve", AI2P[j, :, 0:1], aTi, -1.0, None, ALU.mult)

        def e4(t, n):
            return bc(re(t, "p g (m o) -> p g m o", o=1), [128, t.ap.shape[1], n, 16])

        def b4(t, n):
            return bc(re(t, "p g (o c) -> p g o c", o=1), [128, t.ap.shape[1], n, 16])

        def f4(t):
            return re(t, "p g d (s c) -> p g (d s) c", c=16)

        for fc in range(NFC):
            f0 = fc * 8
            DMA("sp", BR[:], bre_d[j, :, f0:f0 + 8, :])
            DMA("sp", BI[:], bim_d[j, :, f0:f0 + 8, :])
            DMA("sp", CR[:], cre_d[j, :, f0:f0 + 8, :])
            DMA("sp", CI[:], cim_d[j, :, f0:f0 + 8, :])
            pw_tables(Pr, Pi, TAB[LRDT, f0:f0 + 8], TAB[LIDT, f0:f0 + 8], re(EXPT[:], "p a d s -> p (a d s)")[:, 0:MA] if False else V(EXPT.full.rearrange("p a d s -> p (a d s)")[:, 0:MA], EXPT[:].keys),
                      (G8, MA), PWt[0:3] + [PWi])
            TS("dve", Pin[:], Pi[:, 40:80], -1.0, None, ALU.mult)
            cmul_bc(Er[:], Ei[:], Pr[:], Pi[:], bcg(TAB[COR, f0:f0 + 8], MA), bcg(TAB[COI, f0:f0 + 8], MA), ETMP[:])
            for h0 in (0, 4):
                hs_ = slice(h0, h0 + 4)
                TT("dve", f4(LRE[:]), e4(Er[hs_, 0:40], M), b4(BR[hs_, :], M), ALU.mult)
                TT("dve", f4(LT[:]), e4(Ei[hs_, 0:40], M), b4(BI[hs_, :], M), ALU.mult)
                TT("dve", f4(LRE[:]), f4(LRE[:]), f4(LT[:]), ALU.subtract)
                TT("pool", f4(LT2[:]), e4(Ei[hs_, 0:40], M), b4(BR[hs_, :], M), ALU.mult)
                TT("dve", f4(LIM[:]), e4(Er[hs_, 0:40], M), b4(BI[hs_, :], M), ALU.mult)
                TT("dve", f4(LIM[:]), f4(LIM[:]), f4(LT2[:]), ALU.add)
                for g4 in range(4):
                    for ri, LL in enumerate((LRE, LIM)):
                        pb = ps_next(3, 6)
                        for sg in range(4):
                            TR(psv(pb, sg * 128, (sg + 1) * 128), LL[g4, sg, :], IDN[:])
                        CP("act", WIN[h0 + g4, ri], re(psv(pb), "p (s n) -> p s n", s=4))
                mods_tick(1)
            DMA("sp", V(WIN_D[j][:, f0:f0 + 8], [("dr", "win", j, fc)]), WIN[:])
            cmul_e("dve", re(L0R[:], "p g (s c) -> p g s c", c=16), re(L0I[:], "p g (s c) -> p g s c", c=16),
                   e4(Er[:, 80:88], 8), e4(Ei[:, 80:88], 8), b4(BR[:], 8), b4(BI[:], 8), re(L0T[:], "p g (s c) -> p g s c", c=16))
            for h0 in (0, 4):
                hs_ = slice(h0, h0 + 4)
                TT("dve", f4(LRE[:]), b4(CR[hs_, :], M), e4(Pr[hs_, 40:80], M), ALU.mult)
                TT("dve", f4(LT[:]), b4(CI[hs_, :], M), e4(Pi[hs_, 40:80], M), ALU.mult)
                TT("dve", f4(LRE[:]), f4(LRE[:]), f4(LT[:]), ALU.subtract)
                TT("pool", f4(LT2[:]), b4(CI[hs_, :], M), e4(Pr[hs_, 40:80], M), ALU.mult)
                TT("dve", f4(LIM[:]), b4(CR[hs_, :], M), e4(Pin[hs_, :], M), ALU.mult)
                TT("dve", f4(LIM[:]), f4(LIM[:]), f4(LT2[:]), ALU.subtract)
                CP("act", WIN[hs_, 0], LRE[:, 0:4, :])
                CP("act", WIN[hs_, 1], LIM[:, 0:4, :])
                for g4 in range(4):
                    gl = h0 + g4
                    g = f0 + gl
                    pF = ps_next(3, 6)
                    pB = ps_next(3, 6)
                    for dl in range(4):
                        for (pbk, p0, slot) in ((pF, 0, dl), (pB, 64, 4 - dl)):
                            outv = psv(pbk, dl * 128, (dl + 1) * 128)
                            MM(outv, V(L0R.full[p0:p0 + 64, gl, :], L0R[gl].keys), V(LRE.full[p0:p0 + 64, g4, slot, :], LRE[g4].keys), True, False)
                            MM(outv, V(L0I.full[p0:p0 + 64, gl, :], L0I[gl].keys), V(LIM.full[p0:p0 + 64, g4, slot, :], LIM[g4].keys), False, True)
                    TT("dve", D1[:], psv(pF, 0, 128), MSKF[:], ALU.mult)
                    TT("dve", D2[:], psv(pB, 0, 128), MSKB[:], ALU.mult)
                    TT("dve", D1[:], D1[:], D2[:], ALU.add)
                    STT("dve", IM[gl, 0, :], IDN[:], TAB[DST, g:g + 1], D1[:], ALU.mult, ALU.add)
                    CP("act", IM[gl, 1:4, :], re(psv(pF, 128, 512), "p (d x) -> p d x", d=3))
                    CP("act", IM[gl, 4:7, :], re(psv(pB, 128, 512), "p (d x) -> p d x", d=3))
                mods_tick(1)
            DMA("sp", V(WOUT_D[j][:, f0:f0 + 8], [("dr", "wout", j, fc)]), WIN[:])
            DMA("sp", V(IM_D[j][:, f0:f0 + 8], [("dr", "im", j, fc)]), IM[:])

    def s5_block(i, b, tiles, do_ln1=True):
        j = i // 2
        o = ARENA0
        ZT = alloc([NFC, NTOK], BF16, at=o); o += 36864
        HS = alloc([64, KCH + 1, 2], F32, at=o); o += 37376
        PQ = alloc([64, 2], F32, at=o); o += 512
        PQ2 = alloc([64, 2], F32, at=o); o += 512
        PQb = alloc([64, 2], F32, at=o); o += 512
        PQ2b = alloc([64, 2], F32, at=o); o += 512
        HMf = alloc([NTOK], BF16, at=o); o += 4608
        Z0 = alloc([3, 8, 128], BF16, at=o); o += 6144
        U = alloc([8, NKAP], BF16, at=o); o += 4608
        WINS = [alloc([2, 4, 128], BF16, at=o + k * 2048) for k in range(4)]; o += 8192
        IMS = [alloc([7, 128], BF16, at=o + k * 2048) for k in range(2)]; o += 4096
        HB = [alloc([2, KCH], BF16, at=o + k * 512) for k in range(2)]; o += 1024
        YG = alloc([8, NKAP], BF16, at=o); o += 4608
        Z1 = alloc([3, 8, 128], BF16, at=o); o += 6144
        assert o <= ARBYTES, o
        wg0 = ARENA0 + 36864
        WG = alloc([NFC, 2 * D], BF16, at=wg0)
        SIG = [alloc([512], F32, at=wg0 + 32768 + s * 2048) for s in range(2)]
        MIX = [alloc([512], F32, at=wg0 + 36864 + s * 2048) for s in range(2)]
        lnbase = wg0 + 40960
        assert lnbase + 24576 <= ARBYTES
        AR2 = AR2P
        AI2 = AI2P

        v0 = HS.view((slice(None), 0, slice(None)), 64, 0)
        MEMSET("dve", V(v0.ap, v0.keys + ["hsf"]), 0.0)
        v0 = HS.view((slice(None), KCH, slice(None)), 64, 64)
        MEMSET("dve", V(v0.ap, v0.keys + ["hsb"]), 0.0)

        def make_U(fc):
            for (t0, n, who) in ((0, CTX, 2), (CTX, SEQ, b)):
                ACT(HMf[t0:t0 + n], XT[fc, t0:t0 + n], AF.Identity, scale=mod(i, 1, fc, who), bias=mod(i, 0, fc, who))
            for kb, (k0, nk) in enumerate(KBS):
                pb = ps_next(0, 3)
                for tl in range(8):
                    src = HMf[k0 * 8 + tl:k0 * 8 + 8 * nk:8]
                    TR(V(PSB[pb][0:nk, tl * 128:(tl + 1) * 128], [("ps", pb)]), src, IDNB[:])
                z0v = Z0.view((kb, slice(None), slice(None)), nk, 0)
                CP("act", V(z0v.ap.rearrange("p g (t c) -> p g t c", c=16), z0v.keys),
                   V(PSB[pb][0:nk, :].rearrange("p (t g c) -> p g t c", t=8, g=8), [("ps", pb)]))
            for g3 in range(0, 8, 3):
                gn = min(3, 8 - g3)
                pb = ps_next(0, 3)
                for gl in range(g3, g3 + gn):
                    for kb, (k0, nk) in enumerate(KBS):
                        src = Z0.view((kb, gl, slice(None)), nk, 0)
                        c0 = (gl - g3) * NKAP + k0
                        TR(V(PSB[pb][:, c0:c0 + nk], [("ps", pb)]), src, V(IDNB.full[0:nk, 0:nk], IDNB[:].keys))
                CP("dve", U[g3:g3 + gn, :], V(PSB[pb][:, 0:gn * NKAP].rearrange("p (g k) -> p g k", g=gn), [("ps", pb)]))

        nw = 0
        for fc in range(NFC):
            make_U(fc)
            f0 = fc * 8
            for gl in range(8):
                g = f0 + gl
                W = WINS[nw % 4]
                nw += 1
                DMA("sp", W[:], V(WIN_D[j][:, g], [("dr", "win", j, fc)]))
                pb = ps_next(3, 6)
                for ri in range(2):
                    c0 = ri * KCH
                    for sh in range(4):
                        MM(psv(pb, c0, c0 + KCH), W[ri, sh, :], U[gl, sh:NKAP:4], sh == 0, sh == 3)
                src4 = PS[pb][:, 0:2 * KCH].rearrange("p (g r k) -> p g k r", g=1, r=2)
                P.op("act", lambda e, src4=src4, g=g: e.activation(out=HS.full[0:64, g:g + 1, 1:KCH + 1, :], in_=src4[0:64], func=AF.Copy),
                     reads=[("ps", pb)], writes=HS[g:g + 1].keys + ["hsf"])
                P.op("dve", lambda e, src4=src4, g=g: e.tensor_copy(out=HS.full[64:128, g:g + 1, 0:64, :], in_=src4[64:128, :, 8:72, :]),
                     reads=[("ps", pb)], writes=HS[g:g + 1].keys + ["hsb"])
                P.op("dve", lambda e, src4=src4, g=g: e.tensor_copy(out=HS.full[64:128, g:g + 1, 64:72, :], in_=src4[64:128, :, 0:8, :]),
                     reads=[("ps", pb)], writes=HS[g:g + 1].keys + ["hsb"])

        for step in range(KCH):
            for (eng, p0, prevc, curc, pq, pq2) in (("dve", 0, step, step + 1, PQ, PQ2), ("pool", 64, KCH - step, KCH - 1 - step, PQb, PQ2b)):
                prev = HS.full[p0:p0 + 64, :, prevc, :]
                cur = HS.full[p0:p0 + 64, :, curc, :]
                ar2 = AR2.full[p0:p0 + 64, j]
                ai2 = AI2.full[p0:p0 + 64, j]
                a = pq.full[p0:p0 + 64]
                q = pq2.full[p0:p0 + 64]
                hk = ["hsf" if p0 == 0 else "hsb"]
                P.op(eng, lambda e, a=a, ar2=ar2, prev=prev: e.tensor_tensor(out=a, in0=ar2, in1=prev, op=ALU.mult),
                     reads=hk + AR2[j].keys, writes=pq[:].keys)
                P.op(eng, lambda e, q=q, ai2=ai2, prev=prev: e.tensor_tensor(out=q[:, :, 0:1], in0=ai2[:, :, 0:1], in1=prev[:, :, 1:2], op=ALU.mult),
                     reads=hk + AI2[j].keys, writes=pq2[:].keys)
                P.op(eng, lambda e, q=q, ai2=ai2, prev=prev: e.tensor_tensor(out=q[:, :, 1:2], in0=ai2[:, :, 1:2], in1=prev[:, :, 0:1], op=ALU.mult),
                     reads=hk + AI2[j].keys, writes=pq2[:].keys)
                P.op(eng, lambda e, a=a, q=q: e.tensor_tensor(out=a, in0=a, in1=q, op=ALU.add),
                     reads=pq[:].keys + pq2[:].keys, writes=pq[:].keys)
                P.op(eng, lambda e, a=a, cur=cur: e.tensor_tensor(out=cur, in0=cur, in1=a, op=ALU.add),
                     reads=pq[:].keys + hk, writes=hk)

        ng = 0
        for fc in range(NFC):
            make_U(fc)
            f0 = fc * 8
            for gl in range(8):
                g = f0 + gl
                W = WINS[nw % 4]
                nw += 1
                IMg = IMS[ng % 2]
                hb = HB[ng % 2]
                ng += 1
                DMA("sp", W[:], V(WOUT_D[j][:, g], [("dr", "wout", j, fc)]))
                DMA("sp", IMg[:], V(IM_D[j][:, g], [("dr", "im", j, fc)]))
                hsk = HS[g:g + 1].keys + ["hsf", "hsb"]
                hbk = hb[:].keys
                P.op("act", lambda e, g=g, hb=hb: e.activation(out=hb.full[0:64], in_=HS.full[0:64, g, 0:KCH, :].rearrange("p k r -> p r k"), func=AF.Copy),
                     reads=hsk, writes=hbk)
                P.op("dve", lambda e, g=g, hb=hb: e.tensor_copy(out=hb.full[64:128, :, 8:72], in_=HS.full[64:128, g, 1:65, :].rearrange("p k r -> p r k")),
                     reads=hsk, writes=hbk)
                P.op("dve", lambda e, g=g, hb=hb: e.tensor_copy(out=hb.full[64:128, :, 0:8], in_=HS.full[64:128, g, 65:73, :].rearrange("p k r -> p r k")),
                     reads=hsk, writes=hbk)
                py = ps_next(3, 6)
                for th in range(4):
                    outv = V(PS[py][:, th:NKAP:4], [("ps", py)])
                    for sh in range(4):
                        dl = th - sh
                        blk = 0 if dl == 0 else (dl if dl > 0 else 3 - dl)
                        MM(outv, IMg[blk, :], U[gl, sh:NKAP:4], sh == 0, False)
                    MM(outv, W[0, th, :], hb[0, :], False, False)
                    MM(outv, W[1, th, :], hb[1, :], False, True)
                ACT(YG[gl, :], psv(py, 0, NKAP), AF.Gelu)
            for kb, (k0, nk) in enumerate(KBS):
                pb = ps_next(0, 3)
                for gl in range(8):
                    TR(V(PSB[pb][0:nk, gl * 128:(gl + 1) * 128], [("ps", pb)]), YG[gl, k0:k0 + nk], IDNB[:])
                src = PSB[pb][0:nk, :].rearrange("p (g t c) -> p t g c", g=8, t=8)
                dstv = Z1.view((kb, slice(None), slice(None)), nk, 0)
                P.op("act", lambda e, src=src, dstv=dstv: e.activation(out=dstv.ap.rearrange("p t (g c) -> p t g c", g=8), in_=src, func=AF.Copy),
                     reads=[("ps", pb)], writes=dstv.keys)
                pb2 = ps_next(0, 3)
                for tl in range(8):
                    TR(V(PSB[pb2][:, tl * 128:tl * 128 + nk], [("ps", pb2)]), Z1.view((kb, tl, slice(None)), nk, 0), V(IDNB.full[0:nk, 0:nk], IDNB[:].keys))
                zt = ZT[fc, k0 * 8:k0 * 8 + 8 * nk]
                CP("dve", re(zt, "p (k t) -> p t k", t=8), V(PSB[pb2][:, :].rearrange("p (t k) -> p t k", t=8)[:, :, 0:nk], [("ps", pb2)]))

        DMA("pool", WG[:], wglu_d[j].rearrange("(kc p) n -> p kc n", p=128))
        n_ = 0
        for (t0, n) in tiles:
            who = 2 if t0 < CTX else b
            for oc in range(NFC):
                pa = ps_next(0, 3)
                pg = ps_next(3, 6)
                for kc in range(NFC):
                    MM(psv(pa, 0, n), WG[kc, oc * 128:(oc + 1) * 128], ZT[kc, t0:t0 + n], kc == 0, kc == NFC - 1)
                for kc in range(NFC):
                    MM(psv(pg, 0, n), WG[kc, D + oc * 128:D + (oc + 1) * 128], ZT[kc, t0:t0 + n], kc == 0, kc == NFC - 1)
                sg = SIG[n_ % 2]
                mx = MIX[n_ % 2]
                n_ += 1
                ACT(sg[0:n], psv(pg, 0, n), AF.Sigmoid, bias=vcol(c_bglu(j) + 8 + oc))
                STT("dve", mx[0:n], psv(pa, 0, n), vcol(c_bglu(j) + oc), sg[0:n], ALU.add, ALU.mult)
                xv = XT[oc, t0:t0 + n]
                STT("dve", xv, mx[0:n], mod(i, 2, oc, who), xv, ALU.mult, ALU.add)
        if do_ln1:
            layer_norm(tiles, c_lng(i, 0), c_lnb(i, 0), 0, lnbase)

    s5_consts()
    compute_mods()
    mods_tick(8)
    for jj in sorted(set(i // 2 for (i, dm, _) in plan if dm and i % 2 == 0)):
        s5_tables(jj)
    mods_tick(1000)
    for b in range(nseq):
        load_seq(b)
        for (i, do_mix, do_mlp) in plan:
            tiles = TILES if i < 2 else TILES[1:]
            if do_mix:
                if i % 2 == 0:
                    s5_block(i, b, tiles, do_ln1=not do_mlp)
                else:
                    conv_block(i, b, tiles, do_ln1=not do_mlp)
            if do_mlp:
                mlp_block(i, b, tiles, fuse_ln1=do_mix)
        store_seq(b)

    P.emit()
    print('ops/milestones/waits per engine:', P.stats)
    return nc, es


_CACHE = {}


def host_inputs(inputs):
    f = np.float32
    x = np.asarray(inputs["x"], f)
    ctx = np.asarray(inputs["ctx"], f)
    c = np.asarray(inputs["c"], f)
    c_ctx = np.asarray(inputs["c_ctx"], f)

    def fm(v):
        return np.moveaxis(v.reshape(v.shape[:-1] + (8, 128)), -1, 0)

    rows = []
    rows.append(np.asarray(inputs["b_ada"], f).reshape(4 * 48, 128))
    rows.append(np.asarray(inputs["ln_gain"], f).reshape(64, 128))
    rows.append(np.asarray(inputs["ln_bias"], f).reshape(64, 128))
    rows.append(np.asarray(inputs["s5_d"], f).reshape(16, 128))
    rows.append(np.asarray(inputs["s5_b_glu"], f).reshape(32, 128))
    rows.append(np.asarray(inputs["cv_b_pw1"], f).reshape(32, 128))
    rows.append(np.asarray(inputs["cv_w_dw"], f).reshape(2 * 31 * 8, 128))
    rows.append(np.asarray(inputs["cv_b_dw"], f).reshape(16, 128))
    rows.append(np.asarray(inputs["cv_ln_g"], f).reshape(16, 128))
    rows.append(np.asarray(inputs["cv_ln_b"], f).reshape(16, 128))
    rows.append(np.asarray(inputs["cv_b_pw2"], f).reshape(16, 128))
    vec = np.concatenate(rows, axis=0)
    vec = np.concatenate([vec, np.zeros((NVEC - vec.shape[0], 128), f)], axis=0)
    vecT = np.ascontiguousarray(vec.T)

    def dn(v):
        return np.ascontiguousarray(np.transpose(np.asarray(v, f), (0, 1, 3, 2)).reshape(2, 128, 64))

    shared = {
        "vecT": vecT,
        "w_ada": np.asarray(inputs["w_ada"], f), "mlp_w1": np.asarray(inputs["mlp_w1"], f), "mlp_w2": np.asarray(inputs["mlp_w2"], f),
        "s5_w_glu": np.asarray(inputs["s5_w_glu"], f), "cv_w_pw1": np.asarray(inputs["cv_w_pw1"], f),
        "cv_w_pw2": np.asarray(inputs["cv_w_pw2"], f),
        "lam_reT": dn(inputs["s5_lam_re"]), "lam_imT": dn(inputs["s5_lam_im"]),
        "log_dtT": np.ascontiguousarray(np.broadcast_to(np.asarray(inputs["s5_log_dt"], f)[:, :, None, :], (2, 2, 64, 64)).reshape(2, 128, 64)),
        "b_reT": np.ascontiguousarray(np.transpose(np.asarray(inputs["s5_b_re"], f), (0, 1, 3, 2, 4)).reshape(2, 128, 64, 16)),
        "b_imT": np.ascontiguousarray(np.transpose(np.asarray(inputs["s5_b_im"], f), (0, 1, 3, 2, 4)).reshape(2, 128, 64, 16)),
        "c_reT": np.ascontiguousarray(np.transpose(np.asarray(inputs["s5_c_re"], f), (0, 1, 4, 2, 3)).reshape(2, 128, 64, 16)),
        "c_imT": np.ascontiguousarray(np.transpose(np.asarray(inputs["s5_c_im"], f), (0, 1, 4, 2, 3)).reshape(2, 128, 64, 16)),
        "s5_dT": np.ascontiguousarray(np.tile(np.transpose(np.asarray(inputs["s5_d"], f).reshape(2, 64, 16), (0, 2, 1)), (1, 8, 1))),
    }
    maps = []
    nb = x.shape[0] // 2
    for k in range(nb):
        m = dict(shared)
        m["xT"] = np.ascontiguousarray(np.stack([np.transpose(x[2 * k + s].T.reshape(8, 128, SEQ), (1, 0, 2)) for s in range(2)]))
        m["cT"] = np.ascontiguousarray(np.stack([np.transpose(ctx[2 * k + s].T.reshape(8, 128, CTX), (1, 0, 2)) for s in range(2)]))
        cc = np.stack([c[2 * k], c[2 * k + 1], c_ctx], axis=0)
        m["condT"] = np.ascontiguousarray(np.transpose(cc.reshape(3, 8, 128), (2, 1, 0)))
        maps.append(m)
    return maps


def host_output(res_list):
    outs = []
    for r in res_list:
        yT = r["yT"]
        for s in range(2):
            full = np.transpose(yT[s], (1, 0, 2)).reshape(D, NTOK).T
            outs.append(full[CTX:])
    return np.ascontiguousarray(np.stack(outs, axis=0).astype(np.float32))


def kernel(**inputs):
    if "nc" not in _CACHE:
        _CACHE["nc"] = build_nc()
    nc, es = _CACHE["nc"]
    maps = host_inputs(inputs)
    res = run_bass_kernel_spmd(nc, maps, core_ids=list(range(len(maps))))
    return host_output(res.results)
```
